# Optimizing a Trainium2 kernel written in Bass

```python
import jax, jax.numpy as jnp
from jax import lax
import numpy as np

D_MODEL = 2048
BATCH = 4
SEQ = 8192
DEPTH = 1
DEC_BATCH = 1
DEC_SEQ = 8192
PAST_LEN = 128

D_A = D_MODEL
H_A = 16
BW_A = D_A // H_A
CONV_W = 4
RG_C = 8.0
D_B = D_MODEL // 2
G_B = 8
C_B = D_B // G_B
PLE_DIM = 256
N_IN = 2 * D_A + 2 * D_B + 2 * D_MODEL
EPS = 1e-6

kernel_name = 'hawk_fnet_parallel_encoder'


def _rms(x, g):
    x32 = x.astype(jnp.float32)
    y = x32 * lax.rsqrt(jnp.mean(x32 * x32, axis=-1, keepdims=True) + EPS)
    return (y * g.astype(jnp.float32)).astype(x.dtype)


def _lin_combine(e1, e2):
    a1, b1 = e1
    a2, b2 = e2
    return a1 * a2, a2 * b1 + b2


def _centred_dwconv(x, w, b):
    s = x.shape[1]
    xp = jnp.pad(x, ((0, 0), (2, CONV_W - 3), (0, 0)))
    out = b
    for k in range(CONV_W):
        out = out + xp[:, k:k + s, :] * w[k]
    return out


def _rglru_dir(c, w_r, b_r, w_i, b_i, lam, reverse):
    bsz, s, _ = c.shape
    cb = c.reshape(bsz, s, H_A, BW_A)
    r = jax.nn.sigmoid(jnp.einsum('bshi,hij->bshj', cb, w_r).reshape(bsz, s, D_A) + b_r)
    i = jax.nn.sigmoid(jnp.einsum('bshi,hij->bshj', cb, w_i).reshape(bsz, s, D_A) + b_i)
    log_a = (-RG_C * r.astype(jnp.float32)) * jax.nn.softplus(-lam.astype(jnp.float32))
    a = jnp.exp(log_a)
    u = jnp.sqrt(-jnp.expm1(2.0 * log_a)) * (i * c).astype(jnp.float32)
    _, h = lax.associative_scan(_lin_combine, (a, u), reverse=reverse, axis=1)
    return h


def _fourier_mix(xb):
    bsz, s, _ = xb.shape
    xg = xb.astype(jnp.float32).reshape(bsz, s, G_B, C_B)
    y = jnp.fft.fft2(xg, axes=(1, 3), norm='ortho').real
    return y.reshape(bsz, s, D_B).astype(xb.dtype)


def _layer(x, p_l, g_pre, w_in, conv_w, conv_b, w_rgate, b_rgate, w_igate, b_igate, lam,
           w_a, w_b, w_out, g_post, w_ple, w_ple_gate, g_ple):
    h = _rms(x, g_pre)
    u = h @ w_in
    cuts = [D_A, 2 * D_A, 2 * D_A + D_B, 2 * D_A + 2 * D_B, 2 * D_A + 2 * D_B + D_MODEL]
    xa, za, xb, zb, ga, gb = jnp.split(u, cuts, axis=-1)
    c = _centred_dwconv(xa, conv_w, conv_b)
    hf = _rglru_dir(c, w_rgate[0], b_rgate[0], w_igate[0], b_igate[0], lam[0], False)
    hb = _rglru_dir(c, w_rgate[1], b_rgate[1], w_igate[1], b_igate[1], lam[1], True)
    ya = (hf + hb).astype(x.dtype) * jax.nn.silu(za)
    br_a = ya @ w_a
    yb = _fourier_mix(xb) * jax.nn.silu(zb)
    br_b = yb @ w_b
    m = jax.nn.sigmoid(ga) * br_a + jax.nn.sigmoid(gb) * br_b
    x = x + _rms(m @ w_out, g_post)
    e = (p_l @ w_ple) * jax.nn.sigmoid(x @ w_ple_gate)
    return x + _rms(e, g_ple)


def _trunk(x, p, g_pre, w_in, conv_w, conv_b, w_rgate, b_rgate, w_igate, b_igate, lam,
           w_a, w_b, w_out, g_post, w_ple, w_ple_gate, g_ple):
    for l in range(DEPTH):
        x = _layer(x, p[l], g_pre[l], w_in[l], conv_w[l], conv_b[l], w_rgate[l], b_rgate[l],
                   w_igate[l], b_igate[l], lam[l], w_a[l], w_b[l], w_out[l], g_post[l],
                   w_ple[l], w_ple_gate[l], g_ple[l])
    return x


def setup_inputs(seed: int = 0) -> dict:
    key = jax.random.key(seed)
    ks = jax.random.split(key, 24)
    f32 = jnp.float32
    nrm = lambda k, shape, scale: jax.random.normal(k, shape, f32) * scale
    x_prompt = nrm(ks[0], (BATCH, SEQ, D_MODEL), 1.0)
    x_sample = nrm(ks[1], (DEC_BATCH, DEC_SEQ, D_MODEL), 1.0)
    p_prompt = nrm(ks[2], (DEPTH, BATCH, SEQ, PLE_DIM), 1.0)
    p_sample = nrm(ks[3], (DEPTH, DEC_BATCH, DEC_SEQ, PLE_DIM), 1.0)
    g_pre = 1.0 + nrm(ks[4], (DEPTH, D_MODEL), 0.02)
    w_in = nrm(ks[5], (DEPTH, D_MODEL, N_IN), D_MODEL ** -0.5)
    conv_w = nrm(ks[6], (DEPTH, CONV_W, D_A), CONV_W ** -0.5)
    conv_b = nrm(ks[7], (DEPTH, D_A), 0.02)
    w_rgate = nrm(ks[8], (DEPTH, 2, H_A, BW_A, BW_A), BW_A ** -0.5)
    b_rgate = nrm(ks[9], (DEPTH, 2, D_A), 0.02)
    w_igate = nrm(ks[10], (DEPTH, 2, H_A, BW_A, BW_A), BW_A ** -0.5)
    b_igate = nrm(ks[11], (DEPTH, 2, D_A), 0.02)
    a0 = jax.random.uniform(ks[12], (DEPTH, 2, D_A), f32, 0.9, 0.999)
    lam = jnp.log(a0) - jnp.log1p(-a0)
    w_a = nrm(ks[13], (DEPTH, D_A, D_MODEL), D_A ** -0.5)
    w_b = nrm(ks[14], (DEPTH, D_B, D_MODEL), D_B ** -0.5)
    w_out = nrm(ks[15], (DEPTH, D_MODEL, D_MODEL), D_MODEL ** -0.5)
    g_post = 1.0 + nrm(ks[16], (DEPTH, D_MODEL), 0.02)
    w_ple = nrm(ks[17], (DEPTH, PLE_DIM, D_MODEL), PLE_DIM ** -0.5)
    w_ple_gate = nrm(ks[18], (DEPTH, D_MODEL, D_MODEL), D_MODEL ** -0.5)
    g_ple = 1.0 + nrm(ks[19], (DEPTH, D_MODEL), 0.02)
    return {'x_prompt': x_prompt, 'x_sample': x_sample, 'p_prompt': p_prompt, 'p_sample': p_sample,
            'g_pre': g_pre, 'w_in': w_in, 'conv_w': conv_w, 'conv_b': conv_b,
            'w_rgate': w_rgate, 'b_rgate': b_rgate, 'w_igate': w_igate, 'b_igate': b_igate,
            'lam': lam, 'w_a': w_a, 'w_b': w_b, 'w_out': w_out, 'g_post': g_post,
            'w_ple': w_ple, 'w_ple_gate': w_ple_gate, 'g_ple': g_ple}


def reference(x_prompt, x_sample, p_prompt, p_sample, g_pre, w_in, conv_w, conv_b,
              w_rgate, b_rgate, w_igate, b_igate, lam, w_a, w_b, w_out, g_post,
              w_ple, w_ple_gate, g_ple):
    y_prompt = _trunk(x_prompt, p_prompt, g_pre, w_in, conv_w, conv_b, w_rgate, b_rgate,
                      w_igate, b_igate, lam, w_a, w_b, w_out, g_post, w_ple, w_ple_gate, g_ple)
    y_sample = _trunk(x_sample, p_sample, g_pre, w_in, conv_w, conv_b, w_rgate, b_rgate,
                      w_igate, b_igate, lam, w_a, w_b, w_out, g_post, w_ple, w_ple_gate, g_ple)
    return (y_prompt, y_sample)
```

```python
from contextlib import ExitStack

import numpy as np
import ml_dtypes
import concourse.bass as bass
import concourse.mybir as mybir
from concourse.bass_utils import run_bass_kernel_spmd

F32 = mybir.dt.float32
BF16 = mybir.dt.bfloat16
AF = mybir.ActivationFunctionType
ALU = mybir.AluOpType


class Sched:
    ENGS = ("pe", "act", "dve", "pool", "sp")

    def __init__(self, nc):
        self.nc = nc
        self.es = ExitStack()
        self.ops = {e: [] for e in self.ENGS}
        self.cnt = {e: 0 for e in self.ENGS}
        self.dcnt = {}
        self.last_w = {}
        self.readers = {}
        self.waited = {e: {} for e in self.ENGS}
        self.sems = {}

    def sbuf(self, name, shape, dt):
        return self.es.enter_context(self.nc.sbuf_tensor(name, shape, dt))

    def psum(self, name, shape, dt):
        return self.es.enter_context(self.nc.psum_tensor(name, shape, dt))

    def _deps(self, eng, reads, writes):
        need = {}

        def add(k, v):
            if k == "pe" and eng == "pe":
                return
            if need.get(k, 0) < v:
                need[k] = v

        for r in reads:
            t = self.last_w.get(r)
            if t:
                add(*t)
        for w in writes:
            t = self.last_w.get(w)
            if t:
                add(*t)
            for k, v in self.readers.get(w, {}).items():
                add(k, v)
        waits = []
        wd = self.waited[eng]
        for k, v in need.items():
            if wd.get(k, 0) >= v:
                continue
            wd[k] = v
            waits.append((k, v))
        return waits

    def _commit(self, tok, reads, writes):
        k, v = tok
        for r in reads:
            d = self.readers.setdefault(r, {})
            if d.get(k, 0) < v:
                d[k] = v
        for w in writes:
            self.last_w[w] = tok
            self.readers[w] = {}

    def op(self, eng, fn, reads=(), writes=()):
        waits = self._deps(eng, reads, writes)
        self.cnt[eng] += 1
        tok = (eng, self.cnt[eng])
        self.ops[eng].append((waits, fn, True))
        self._commit(tok, reads, writes)

    def dma(self, q, out, in_, reads=(), writes=(), key=None, **kw):
        pairs = out if isinstance(out, list) else [(out, in_)]
        waits = self._deps(q, reads, writes)
        dk = "d:" + key
        self.dcnt[dk] = self.dcnt.get(dk, 0) + 16 * len(pairs)
        tok = (dk, self.dcnt[dk])

        def fn(e, pairs=pairs, dk=dk, kw=kw):
            for o, i in pairs:
                e.dma_start(out=o, in_=i, **kw).then_inc(self.sems[dk], 16)

        self.ops[q].append((waits, fn, False))
        self._commit(tok, reads, writes)

    def fence(self):
        allw = [(e, self.cnt[e]) for e in self.ENGS if self.cnt[e] > 0] + list(self.dcnt.items())
        for eng in self.ENGS:
            waits = []
            wd = self.waited[eng]
            for k, v in allw:
                if k == eng:
                    continue
                if wd.get(k, 0) >= v:
                    continue
                wd[k] = v
                waits.append((k, v))
            if waits:
                self.ops[eng].append((waits, None, False))
        self.last_w = {}
        self.readers = {}

    def finish(self):
        nc = self.nc
        waits = [(k, v) for k, v in self.dcnt.items()] + [(e, self.cnt[e]) for e in self.ENGS if self.cnt[e] > 0 and e != "sp"]
        self.ops["sp"].append((waits, None, False))
        for e in self.ENGS:
            self.sems[e] = self.es.enter_context(nc.semaphore("sem_" + e))
        for i, dk in enumerate(self.dcnt):
            self.sems[dk] = self.es.enter_context(nc.semaphore("semd_%d" % i))

        def emit(name):
            def f(e):
                for waits, fn, inc in self.ops[name]:
                    for k, v in waits:
                        e.wait_ge(self.sems[k], v)
                    if fn is None:
                        continue
                    ins = fn(e)
                    if inc:
                        ins.then_inc(self.sems[name], 1)
            return f

        with nc.Block() as block:
            block.tensor(emit("pe"))
            block.scalar(emit("act"))
            block.vector(emit("dve"))
            block.gpsimd(emit("pool"))
            block.sync(emit("sp"))
        self.es.close()


S = 8192
D = 2048
NIN = 10240
EPS = 1e-6
NV = 176
V_CW, V_CB, V_BR, V_BI, V_LAM = 0, 64, 80, 112, 144


class Arena:
    def __init__(self, s, nbytes):
        self.t = s.sbuf("arena", [128, nbytes // 2], BF16)
        self.cap = nbytes // 2
        self.off = 0

    def mark(self):
        return self.off

    def reset(self, m):
        self.off = m

    def alloc(self, shape, dt):
        n = 1
        for d in shape:
            n *= d
        nel = n * 2 if dt == F32 else n
        nel = (nel + 15) // 16 * 16
        assert self.off + nel <= self.cap, ("arena overflow", self.off, nel, self.cap)
        ap = self.t[:, self.off:self.off + nel]
        self.off += nel
        if dt == F32:
            ap = ap.bitcast(F32)
        ap = ap[:, 0:n]
        if len(shape) == 2:
            ap = ap.rearrange("p (a b) -> p a b", a=shape[0])
        elif len(shape) == 3:
            ap = ap.rearrange("p (a b c) -> p a b c", a=shape[0], b=shape[1])
        elif len(shape) == 4:
            ap = ap.rearrange("p (a b c d) -> p a b c d", a=shape[0], b=shape[1], c=shape[2])
        return ap


def build_program(debug=False, phases=(0, 1, 2, 3, 4, 5), p3="AdaBC", p3g=8):
    nc = bass.Bass("TRN2", target_bir_lowering=False)
    ikind = "ExternalInput"
    skind = "ExternalOutput" if debug else "Internal"

    def din(name, shape, dt=F32):
        return nc.dram_tensor(name, shape, dt, kind=ikind).ap()

    def dscr(name, shape, dt=BF16):
        return nc.dram_tensor(name, shape, dt, kind=skind).ap()

    x = din("x", [S, D])
    p = din("p", [S, 256])
    w_in = din("w_in", [D, NIN])
    w_a = din("w_a", [D, D])
    w_b = din("w_b", [1024, D])
    w_out = din("w_out", [D, D])
    w_gate = din("w_gate", [D, D])
    w_ple = din("w_ple", [256, D])
    w_rg = din("w_rg", [4096, 128])
    w_ig = din("w_ig", [4096, 128])
    vecs = din("vecs", [128, NV])
    gpre_bc = din("gpre_bc", [128, D])
    gpost_bc = din("gpost_bc", [128, D])
    gple_bc = din("gple_bc", [128, D])
    ident = din("ident", [128, 128], BF16)
    f64t = din("f64t", [64, 128], BF16)
    cosG = din("cosG", [128, S], BF16)
    sinG = din("sinG", [128, S], BF16)
    nsinG = din("nsinG", [128, S], BF16)
    ccs = din("ccs", [128, 256], BF16)
    y = nc.dram_tensor("y", [S, D], F32, kind="ExternalOutput").ap()

    win_bf = dscr("win_bf", [D, NIN])
    wa_bf = dscr("wa_bf", [D, D])
    wb_bf = dscr("wb_bf", [1024, D])
    wout_bf = dscr("wout_bf", [D, D])
    wgate_bf = dscr("wgate_bf", [D, D])
    wple_bf = dscr("wple_bf", [256, D])
    wrg_bf = dscr("wrg_bf", [4096, 128])
    wig_bf = dscr("wig_bf", [4096, 128])
    XA_s = dscr("XA_s", [D, S])
    ZA_s = dscr("ZA_s", [D, S])
    XB_s = dscr("XB_s", [8, S, 128])
    ZB_s = dscr("ZB_s", [1024, S])
    GA_s = dscr("GA_s", [D, S])
    GB_s = dscr("GB_s", [D, S])
    YA_s = dscr("YA_s", [D, S])
    YB_s = dscr("YB_s", [1024, S])

    s = Sched(nc)
    A = Arena(s, 204 * 1024)
    PS = [s.psum("psb%d" % i, [128, 512], F32) for i in range(8)]

    def ps(i):
        return PS[i][:]

    def psbf(i):
        return PS[i][:].bitcast(BF16)

    vec_t = A.alloc([NV], F32)
    idt = A.alloc([128], BF16)
    hbr = A.alloc([32], F32)
    hbi = A.alloc([32], F32)
    sc4 = A.alloc([32], F32)
    mhalf = A.alloc([4], F32)
    s.dma("sp", vec_t, vecs, writes=["vec"], key="c0")
    s.dma("sp", idt, ident, writes=["idt"], key="c1")
    s.op("dve", lambda e: e.memset(mhalf, -0.5), writes=["mhalf"])
    s.op("dve", lambda e: e.tensor_scalar(out=hbr, in0=vec_t[:, V_BR:V_BR + 32], scalar1=0.5, scalar2=None, op0=ALU.mult),
         reads=["vec"], writes=["hbr"])
    s.op("dve", lambda e: e.tensor_scalar(out=hbi, in0=vec_t[:, V_BI:V_BI + 32], scalar1=0.5, scalar2=None, op0=ALU.mult),
         reads=["vec"], writes=["hbi"])
    s.op("act", lambda e: e.activation(out=sc4, in_=vec_t[:, V_LAM:V_LAM + 32], func=AF.Exp, scale=-1.0), reads=["vec"], writes=["sc4"])
    s.op("act", lambda e: e.activation(out=sc4, in_=sc4, func=AF.Ln, bias=1.0), reads=["sc4"], writes=["sc4"])
    s.op("dve", lambda e: e.tensor_scalar(out=sc4, in0=sc4, scalar1=-4.0, scalar2=None, op0=ALU.mult), reads=["sc4"], writes=["sc4"])
    base_mark = A.mark()

    def tiled(t, nkb):
        return t.rearrange("a b -> (a b)").rearrange("(g kb kp k8 n) -> g kb kp k8 n", kb=nkb, kp=128, k8=8, n=512)

    win_t = win_bf.rearrange("a b -> (a b)").rearrange("(g kp kc n) -> g kp kc n", kp=128, kc=16, n=512)
    wa_t, wb_t, wo_t, wg_t = tiled(wa_bf, 2), tiled(wb_bf, 1), tiled(wout_bf, 2), tiled(wgate_bf, 2)

    def ph0():
        def cast(dst, src, rows, cols):
            cw = min(cols, 2048)
            nseg = cols // cw
            rstep = 128 if nseg > 1 else 512
            pairs = []
            for r0 in range(0, rows, rstep):
                r1 = min(rows, r0 + rstep)
                if nseg > 1:
                    o = dst[r0:r1, :].rearrange("r (g c) -> r g c", g=nseg)
                    i = src[r0:r1, :].rearrange("r (g c) -> r g c", g=nseg)
                else:
                    o = dst[r0:r1, :]
                    i = src[r0:r1, :]
                pairs.append((o, i))
            s.dma("pool", pairs, None, writes=["W"], key="cast")

        sv_in = w_in.rearrange("(kc kp) (g n) -> g kp kc n", kp=128, n=512)
        s.dma("pool", [(win_t[g], sv_in[g]) for g in range(20)], None, writes=["W"], key="cast")
        def cast_tiled(dst_t, src, nkb):
            sv = src.rearrange("(kb k8 kp) (g n) -> g kb kp k8 n", k8=8, kp=128, n=512)
            pairs = [(dst_t[g, kb], sv[g, kb]) for g in range(4) for kb in range(nkb)]
            s.dma("pool", pairs, None, writes=["W"], key="cast")

        cast_tiled(wa_t, w_a, 2)
        cast_tiled(wb_t, w_b, 1)
        cast_tiled(wo_t, w_out, 2)
        cast_tiled(wg_t, w_gate, 2)
        cast(wple_bf, w_ple, 256, D)
        cast(wrg_bf, w_rg, 4096, 128)
        cast(wig_bf, w_ig, 4096, 128)
        s.fence()

    def ph1():
        A.reset(base_mark)
        gpre_t = A.alloc([D], F32)
        xts = [A.alloc([D], F32) for _ in range(3)]
        hbs = [A.alloc([D], BF16) for _ in range(2)]
        junk = A.alloc([D], BF16)
        TT1 = 1024
        NSUB = TT1 // 128
        hTs = [A.alloc([16, TT1], BF16) for _ in range(2)]
        wts = [A.alloc([16, 512], BF16) for _ in range(3)]
        stg = [A.alloc([4, TT1], BF16) for _ in range(4)]
        sst = [A.alloc([NSUB], F32) for _ in range(2)]
        tmt = [A.alloc([NSUB], F32) for _ in range(2)]
        rst = [A.alloc([NSUB], F32) for _ in range(2)]
        s.dma("sp", gpre_t, gpre_bc, writes=["gpre"], key="c2")
        win_v = win_bf.rearrange("(kc kp) n -> kp kc n", kp=128)
        cnt = {"x": 0, "w": 0, "stg": 0, "mm": 0, "tr": 0}
        order = [(g, "xa") for g in range(0, 4)] + [(8, "xb"), (9, "xb")] + [(g, "za") for g in range(4, 8)] + \
                [(10, "zb"), (11, "zb")] + [(g, "ga") for g in range(12, 16)] + [(g, "gb") for g in range(16, 20)]
        dst_fm = {"xa": (XA_s, 0), "za": (ZA_s, 4), "zb": (ZB_s, 10), "ga": (GA_s, 12), "gb": (GB_s, 16)}
        funcs = {"xa": None, "xb": None, "za": AF.Silu, "zb": AF.Silu, "ga": AF.Sigmoid, "gb": AF.Sigmoid}

        def prep(tt):
            t0 = tt * TT1
            hT = hTs[tt % 2]
            ss, tm, rs = sst[tt % 2], tmt[tt % 2], rst[tt % 2]
            for j in range(NSUB):
                xi = cnt["x"] % 3
                hi = cnt["x"] % 2
                cnt["x"] += 1
                xt, hb = xts[xi], hbs[hi]
                s.dma("sp", xt, x[t0 + j * 128:t0 + (j + 1) * 128, :], writes=[("xt", xi)], key="x%d" % xi)
                s.op("act", lambda e, xt=xt, ss=ss, j=j: e.activation(out=junk, in_=xt, func=AF.Square, accum_out=ss[:, j:j + 1]),
                     reads=[("xt", xi)], writes=["junk", ("ss", tt % 2, j)])
                s.op("dve", lambda e, ss=ss, tm=tm, j=j: e.tensor_scalar(out=tm[:, j:j + 1], in0=ss[:, j:j + 1], scalar1=1.0 / D, scalar2=EPS,
                                                                     op0=ALU.mult, op1=ALU.add),
                     reads=[("ss", tt % 2, j)], writes=[("tm", tt % 2, j)])
                s.op("pool", lambda e, rs=rs, tm=tm, j=j: e.tensor_tensor(out=rs[:, j:j + 1], in0=tm[:, j:j + 1], in1=mhalf[:, 0:1], op=ALU.pow),
                     reads=[("tm", tt % 2, j), "mhalf"], writes=[("rs", tt % 2, j)])
                s.op("dve", lambda e, xt=xt, hb=hb, rs=rs, j=j: e.scalar_tensor_tensor(out=hb, in0=xt, scalar=rs[:, j:j + 1], in1=gpre_t,
                                                                                    op0=ALU.mult, op1=ALU.mult),
                     reads=[("xt", xi), ("rs", tt % 2, j), "gpre"], writes=[("hb", hi)])
                for half in range(2):
                    bank = 6 + cnt["tr"] % 2
                    cnt["tr"] += 1

                    def trf(e, hb=hb, half=half, bank=bank):
                        ins = None
                        pb = psbf(bank)
                        for k in range(8):
                            kc = half * 8 + k
                            ins = e.transpose(pb[:, k * 128:(k + 1) * 128], hb[:, kc * 128:(kc + 1) * 128], idt)
                        return ins

                    s.op("pe", trf, reads=[("hb", hi), "idt"], writes=[("ps", bank)])
                    s.op("dve", lambda e, hT=hT, half=half, bank=bank, j=j: e.tensor_copy(
                        out=hT[:, half * 8:(half + 1) * 8, j * 128:(j + 1) * 128],
                        in_=psbf(bank).rearrange("p (k t) -> p k t", k=8)),
                        reads=[("ps", bank)], writes=[("hT", tt % 2)])

        def groups(tt, lo, hi_):
            t0 = tt * TT1
            hT = hTs[tt % 2]
            for (g, kind) in order[lo:hi_]:
                wi = cnt["w"] % 3
                cnt["w"] += 1
                wt = wts[wi]
                s.dma("sp", wt, win_t[g], reads=["W"], writes=[("wt", wi)], key="w%d" % wi)
                si = cnt["stg"] % 4
                if kind != "xb":
                    cnt["stg"] += 1
                    st = stg[si]
                    for c4 in range(4):
                        pr = cnt["mm"] % 3
                        cnt["mm"] += 1
                        bks = (2 * pr, 2 * pr + 1)

                        def mm(e, wt=wt, hT=hT, c4=c4, bks=bks):
                            ins = None
                            for kc in range(16):
                                for th in range(2):
                                    ins = e.matmul(ps(bks[th]), lhsT=wt[:, kc, c4 * 128:(c4 + 1) * 128], rhs=hT[:, kc, th * 512:(th + 1) * 512],
                                                   start=(kc == 0), stop=(kc == 15))
                            return ins

                        s.op("pe", mm, reads=[("wt", wi), ("hT", tt % 2)], writes=[("ps", bks[0]), ("ps", bks[1])])
                        fn = funcs[kind]
                        for th in range(2):
                            bank = bks[th]
                            if fn is None:
                                s.op("dve", lambda e, st=st, c4=c4, bank=bank, th=th: e.tensor_copy(out=st[:, c4, th * 512:(th + 1) * 512], in_=ps(bank)),
                                     reads=[("ps", bank)], writes=[("stg", si)])
                            else:
                                s.op("act", lambda e, st=st, c4=c4, bank=bank, fn=fn, th=th: e.activation(out=st[:, c4, th * 512:(th + 1) * 512], in_=ps(bank), func=fn),
                                     reads=[("ps", bank)], writes=[("stg", si)])
                    dst, g0 = dst_fm[kind]
                    c0 = (g - g0) * 512
                    s.dma("pool", dst[c0:c0 + 512, t0:t0 + TT1].rearrange("(j p) t -> p j t", p=128), st,
                          reads=[("stg", si)], writes=[], key="st%d" % si)
                else:
                    cb = g - 8
                    for j in range(NSUB):
                        si = cnt["stg"] % 4
                        cnt["stg"] += 1
                        st = stg[si]
                        bank = 2 * (cnt["mm"] % 3) + (j % 2)
                        if j % 2 == 1:
                            cnt["mm"] += 1

                        def mm(e, wt=wt, hT=hT, j=j, bank=bank):
                            ins = None
                            for kc in range(16):
                                ins = e.matmul(ps(bank), lhsT=hT[:, kc, j * 128:(j + 1) * 128], rhs=wt[:, kc, :],
                                               start=(kc == 0), stop=(kc == 15))
                            return ins

                        s.op("pe", mm, reads=[("wt", wi), ("hT", tt % 2)], writes=[("ps", bank)])
                        s.op("dve", lambda e, st=st, bank=bank: e.tensor_copy(out=st[:, 0, 0:512], in_=ps(bank)),
                             reads=[("ps", bank)], writes=[("stg", si)])
                        s.dma("pool", XB_s[cb * 4:(cb + 1) * 4, t0 + j * 128:t0 + (j + 1) * 128, :].rearrange("g p c -> p g c"),
                              st[:, 0, 0:512].rearrange("p (b c) -> p b c", c=128), reads=[("stg", si)], writes=[], key="st%d" % si)

        NT = S // TT1
        prep(0)
        for tt in range(NT):
            groups(tt, 0, 10)
            if tt + 1 < NT:
                prep(tt + 1)
            groups(tt, 10, 20)
        s.fence()

    def ph2():
        A.reset(base_mark)
        xa_pad = A.alloc([S + 4], BF16)
        zsq = [A.alloc([2048], BF16) for _ in range(2)]
        c2 = [A.alloc([S], BF16) for _ in range(2)]
        hf = A.alloc([S], F32)
        NWS = 3
        wsA = [A.alloc([2048], F32) for _ in range(NWS)]
        wsS = [A.alloc([2048], F32) for _ in range(NWS)]
        wsU = [A.alloc([2048], F32) for _ in range(NWS)]
        hbq = [A.alloc([2048], F32) for _ in range(2)]
        yst = [A.alloc([2048], BF16) for _ in range(2)]
        gw = [A.alloc([4, 128], BF16) for _ in range(2)]
        dg = [A.alloc([4, 128], BF16) for _ in range(2)]
        s.op("dve", lambda e: e.memset(xa_pad[:, 0:2], 0.0), writes=["xa"])
        s.op("dve", lambda e: e.memset(xa_pad[:, S + 2:S + 4], 0.0), writes=["xa"])
        cnt = {"u": 0, "gs": 0, "cv": 0, "ev": 0}
        units = []

        def pro_load(h):
            hb2 = h % 2
            s.dma("sp", xa_pad[:, 2:S + 2], XA_s[h * 128:(h + 1) * 128, :], writes=["xa"], key="xa")
            pairs = []
            for d in range(2):
                pairs.append((gw[hb2][:, d * 2 + 0, :], wrg_bf[(d * 16 + h) * 128:(d * 16 + h + 1) * 128, :]))
                pairs.append((gw[hb2][:, d * 2 + 1, :], wig_bf[(d * 16 + h) * 128:(d * 16 + h + 1) * 128, :]))
            s.dma("sp", pairs, None, writes=[("gw", hb2)], key="gw%d" % hb2)
            for k in range(4):
                s.op("dve", lambda e, k=k: e.tensor_scalar(out=dg[hb2][:, k, :], in0=idt, scalar1=vec_t[:, V_CW + k * 16 + h:V_CW + k * 16 + h + 1],
                                                      scalar2=None, op0=ALU.mult),
                     reads=["idt", "vec"], writes=[("dg", hb2)])

        def pro_conv(h, chs=range(16)):
            hb2 = h % 2
            cbuf = c2[hb2]
            for ch in chs:
                bank = cnt["cv"] % 2
                cnt["cv"] += 1

                def cv(e, ch=ch, bank=bank):
                    ins = None
                    for k in range(4):
                        ins = e.matmul(ps(bank), lhsT=dg[hb2][:, k, :], rhs=xa_pad[:, ch * 512 + k:ch * 512 + k + 512], start=(k == 0), stop=(k == 3))
                    return ins

                s.op("pe", cv, reads=[("dg", hb2), "xa"], writes=[("ps", bank)])
                if True:
                    s.op("dve", lambda e, ch=ch, bank=bank: e.tensor_scalar(out=cbuf[:, ch * 512:(ch + 1) * 512], in0=ps(bank),
                                                                     scalar1=vec_t[:, V_CB + h:V_CB + h + 1], scalar2=None, op0=ALU.add),
                         reads=[("ps", bank), "vec"], writes=[("c2", hb2, ch // 4)])
                else:
                    s.op("act", lambda e, ch=ch, bank=bank: e.activation(out=cbuf[:, ch * 512:(ch + 1) * 512], in_=ps(bank), func=AF.Identity,
                                                                  bias=vec_t[:, V_CB + h:V_CB + h + 1]),
                         reads=[("ps", bank), "vec"], writes=[("c2", hb2, ch // 4)])

        def make_unit(h, d, q, u):
            hb2 = h % 2
            cbuf = c2[hb2]
            vi = d * 16 + h
            ws = u % NWS
            Aw, Sw, Uw = wsA[ws], wsS[ws], wsU[ws]

            def s1():
                for cc in range(4):
                    slot = cnt["gs"] % 3
                    cnt["gs"] += 1
                    b0, b1 = 2 + slot * 2, 3 + slot * 2
                    tok0 = q * 2048 + cc * 512

                    def gm(e, b0=b0, b1=b1, tok0=tok0):
                        e.matmul(ps(b0), lhsT=gw[hb2][:, d * 2 + 0, :], rhs=cbuf[:, tok0:tok0 + 512], start=True, stop=True)
                        return e.matmul(ps(b1), lhsT=gw[hb2][:, d * 2 + 1, :], rhs=cbuf[:, tok0:tok0 + 512], start=True, stop=True)

                    s.op("pe", gm, reads=[("gw", hb2), ("c2", hb2, q)], writes=[("ps", b0), ("ps", b1)])
                    s.op("act", lambda e, cc=cc, b0=b0: e.activation(out=Aw[:, cc * 512:(cc + 1) * 512], in_=ps(b0), func=AF.Tanh,
                                                               scale=0.5, bias=hbr[:, vi:vi + 1]),
                         reads=[("ps", b0), "hbr"], writes=[("A", ws)])
                    s.op("act", lambda e, cc=cc, b1=b1: e.activation(out=Uw[:, cc * 512:(cc + 1) * 512], in_=ps(b1), func=AF.Tanh,
                                                               scale=0.5, bias=hbi[:, vi:vi + 1]),
                         reads=[("ps", b1), "hbi"], writes=[("U", ws)])
                s.op("act", lambda e: e.activation(out=Aw, in_=Aw, func=AF.Exp, scale=sc4[:, vi:vi + 1], bias=sc4[:, vi:vi + 1]),
                     reads=[("A", ws), "sc4"], writes=[("A", ws)])
                s.op("act", lambda e: e.activation(out=Sw, in_=Aw, func=AF.Square), reads=[("A", ws)], writes=[("S", ws)])
                s.op("dve", lambda e: e.scalar_tensor_tensor(out=Uw, in0=Uw, scalar=1.0, in1=cbuf[:, q * 2048:(q + 1) * 2048],
                                                             op0=ALU.add, op1=ALU.mult),
                     reads=[("U", ws), ("c2", hb2, q)], writes=[("U", ws)])

            def s2():
                s.op("act", lambda e: e.activation(out=Sw, in_=Sw, func=AF.Sqrt, scale=-1.0, bias=1.0), reads=[("S", ws)], writes=[("S", ws)])
                s.op("dve", lambda e: e.scalar_tensor_tensor(out=Uw, in0=Uw, scalar=0.5, in1=Sw, op0=ALU.mult, op1=ALU.mult),
                     reads=[("U", ws), ("S", ws)], writes=[("U", ws)])
                if d == 0:
                    init = 0.0 if q == 0 else hf[:, q * 2048 - 1:q * 2048]
                    rd = [("A", ws), ("U", ws)] + ([] if q == 0 else [("hf", q - 1)])
                    s.op("dve", lambda e: e.tensor_tensor_scan(out=hf[:, q * 2048:(q + 1) * 2048], data0=Aw, data1=Uw,
                                                               initial=init, op0=ALU.mult, op1=ALU.add),
                         reads=rd, writes=[("hf", q)])
                else:
                    hi = u % 2
                    hq = hbq[hi]
                    init = 0.0 if q == 3 else hbq[1 - hi][:, 0:1]
                    rd = [("A", ws), ("U", ws)] + ([] if q == 3 else [("hbq", 1 - hi)])
                    s.op("dve", lambda e: e.tensor_tensor_scan(out=hq[:, ::-1], data0=Aw[:, ::-1], data1=Uw[:, ::-1],
                                                               initial=init, op0=ALU.mult, op1=ALU.add),
                         reads=rd, writes=[("hbq", hi)])
                    zi_ = u % 2
                    s.dma("sp", zsq[zi_], ZA_s[h * 128:(h + 1) * 128, q * 2048:(q + 1) * 2048], writes=[("zsq", zi_)], key="zs%d" % zi_)
                    s.op("pool", lambda e: e.tensor_tensor(out=Uw, in0=hq, in1=hf[:, q * 2048:(q + 1) * 2048], op=ALU.add),
                         reads=[("hbq", hi), ("hf", q)], writes=[("U", ws)])
                    s.op("pool", lambda e: e.tensor_tensor(out=yst[zi_], in0=Uw, in1=zsq[zi_], op=ALU.mult),
                         reads=[("U", ws), ("zsq", zi_)], writes=[("yst", zi_)])
                    s.dma("pool", YA_s[h * 128:(h + 1) * 128, q * 2048:(q + 1) * 2048], yst[zi_], reads=[("yst", zi_)], writes=[], key="ys%d" % zi_)

            return s1, s2

        for h in range(16):
            first = True
            for d in range(2):
                qs = [0, 1, 2, 3] if d == 0 else [3, 2, 1, 0]
                for q in qs:
                    u = cnt["u"]
                    cnt["u"] += 1
                    s1, s2 = make_unit(h, d, q, u)
                    units.append((h if first else None, s1, s2))
                    first = False
        CONV_SPLIT = {1: [0, 1, 2], 2: [3, 4, 5], 3: [6, 7], 4: [8, 9], 5: [10, 11], 6: [12, 13], 7: [14, 15]}
        pro_load(0)
        pro_conv(0)
        pro_load(1)
        for i, (hp, s1, s2) in enumerate(units):
            hh, idx = i // 8, i % 8
            s1()
            if idx >= 1 and hh + 1 < 16:
                pro_conv(hh + 1, CONV_SPLIT[idx])
            if idx == 7 and hh + 2 < 16:
                pro_load(hh + 2)
            if i >= 1:
                units[i - 1][2]()
        units[-1][2]()
        s.fence()

    def ph3():
        A.reset(base_mark)
        Xg = A.alloc([128, 128], BF16)
        Zre = A.alloc([64, 128], BF16)
        Zim = A.alloc([64, 128], BF16)
        cG = A.alloc([64, 128], BF16)
        sG = A.alloc([64, 128], BF16)
        nG = A.alloc([64, 128], BF16)
        ccs_t = A.alloc([256], BF16)
        f64_t = A.alloc([128], BF16)
        Yt = [A.alloc([4, 2, 128], BF16) for _ in range(2)]
        szb = A.alloc([S], BF16)
        ybt = A.alloc([S], BF16)
        s.dma("sp", cG.rearrange("p a b -> p (a b)"), cosG, writes=["cG"], key="t0")
        s.dma("sp", sG.rearrange("p a b -> p (a b)"), sinG, writes=["sG"], key="t1")
        s.dma("sp", nG.rearrange("p a b -> p (a b)"), nsinG, writes=["nG"], key="t2")
        s.dma("sp", ccs_t, ccs, writes=["ccs"], key="t3")
        s.dma("sp", f64_t[0:64, :], f64t, writes=["f64"], key="t4")
        cnt = {"a": 0, "b": 0, "c": 0, "ev": 0}
        for g in range(p3g):
            s.dma("sp", Xg[0:64].rearrange("p b c -> p (b c)"), XB_s[g].rearrange("(a b) c -> a (b c)", a=64), writes=["Xg"], key="xg")
            s.dma("sp", szb, ZB_s[g * 128:(g + 1) * 128, :], writes=["szb"], key="szb")
            for c4 in range(32 if "A" in p3 else 0):
                bank = cnt["a"] % 2
                cnt["a"] += 1

                def sa(e, c4=c4, bank=bank):
                    ins = None
                    for cc in range(4):
                        c = c4 * 4 + cc
                        ins = e.matmul(ps(bank)[:, cc * 128:(cc + 1) * 128], lhsT=Xg[0:64, :, c], rhs=f64_t[0:64, :], start=True, stop=True)
                    return ins

                s.op("pe", sa, reads=["Xg", "f64"], writes=[("ps", bank)])
                pv = ps(bank).rearrange("p (cc ri a) -> p ri a cc", cc=4, ri=2, a=64)
                if "d" in p3:
                    s.op("dve", lambda e, c4=c4, pv=pv: e.tensor_copy(out=Zre[:, :, c4 * 4:(c4 + 1) * 4], in_=pv[:, 0]),
                         reads=[("ps", bank)], writes=["Zre"])
                if "a" in p3:
                    s.op("dve", lambda e, c4=c4, pv=pv: e.tensor_copy(out=Zim[:, :, c4 * 4:(c4 + 1) * 4], in_=pv[:, 1]),
                         reads=[("ps", bank)], writes=["Zim"])
            for a2 in range(32 if "B" in p3 else 0):
                bank = 2 + cnt["b"] % 4
                cnt["b"] += 1

                def sb(e, a2=a2, bank=bank):
                    ins = None
                    for ai in range(2):
                        ap_ = 2 * a2 + ai
                        o = ai * 256
                        e.matmul(ps(bank)[:, o:o + 128], lhsT=Zre[:, ap_, :], rhs=cG[:, ap_, :], start=True, stop=False)
                        e.matmul(ps(bank)[:, o:o + 128], lhsT=Zim[:, ap_, :], rhs=sG[:, ap_, :], start=False, stop=True)
                        e.matmul(ps(bank)[:, o + 128:o + 256], lhsT=Zre[:, ap_, :], rhs=nG[:, ap_, :], start=True, stop=False)
                        ins = e.matmul(ps(bank)[:, o + 128:o + 256], lhsT=Zim[:, ap_, :], rhs=cG[:, ap_, :], start=False, stop=True)
                    return ins

                s.op("pe", sb, reads=["Zre", "Zim", "cG", "sG", "nG"], writes=[("ps", bank)])
                a4 = a2 // 2
                yi = a4 % 2
                Y = Yt[yi]
                half = a2 % 2
                if a2 % 2 == 0:
                    s.op("dve", lambda e, Y=Y, bank=bank, half=half: e.tensor_copy(out=Y[:, half * 2:half * 2 + 2].rearrange("p a r b -> p (a r b)"), in_=ps(bank)),
                         reads=[("ps", bank)], writes=[("Y", yi)])
                else:
                    s.op("act", lambda e, Y=Y, bank=bank, half=half: e.activation(out=Y[:, half * 2:half * 2 + 2].rearrange("p a r b -> p (a r b)"), in_=ps(bank), func=AF.Copy),
                         reads=[("ps", bank)], writes=[("Y", yi)])
                if a2 % 2 == 1 and "C" in p3:
                    cbank = 6 + cnt["c"] % 2
                    cnt["c"] += 1

                    def sc(e, Y=Y, cbank=cbank):
                        ins = None
                        for al in range(4):
                            e.matmul(ps(cbank)[:, al * 128:(al + 1) * 128], lhsT=ccs_t[:, 0:128], rhs=Y[:, al, 0, :], start=True, stop=False)
                            ins = e.matmul(ps(cbank)[:, al * 128:(al + 1) * 128], lhsT=ccs_t[:, 128:256], rhs=Y[:, al, 1, :], start=False, stop=True)
                        return ins

                    s.op("pe", sc, reads=[("Y", yi), "ccs"], writes=[("ps", cbank)])
                    zv = szb.rearrange("p (b a) -> p a b", a=64)[:, a4 * 4:a4 * 4 + 4, :]
                    ov = ybt.rearrange("p (b a) -> p a b", a=64)[:, a4 * 4:a4 * 4 + 4, :]
                    s.op("dve", lambda e, zv=zv, ov=ov, cbank=cbank: e.tensor_tensor(out=ov, in0=ps(cbank).rearrange("p (a b) -> p a b", a=4), in1=zv, op=ALU.mult),
                         reads=[("ps", cbank), "szb"], writes=["ybt"])
            s.dma("pool", YB_s[g * 128:(g + 1) * 128, :], ybt, reads=["ybt"], writes=[], key="yb")
        s.fence()

    X1_s = dscr("X1_s", [S, D], F32)

    class Ring:
        def __init__(self, nslots):
            self.slots = [A.alloc([8, 512], BF16) for _ in range(nslots)]
            self.n = nslots
            self.c = 0

        def load(self, view, nk, c0, tag):
            ids = []
            for b in range(nk // 8):
                i = self.c % self.n
                self.c += 1
                s.dma("sp", self.slots[i], view[c0 // 512, b], reads=["W"], writes=[("ws", i)], key="%s%d" % (tag, i))
                ids.append(i)
            return ids

        def w(self, ids, kc):
            return self.slots[ids[kc // 8]][:, kc % 8, :]

        def res(self, ids):
            return [("ws", i) for i in ids]

    EPS4 = EPS
    def ph4():
        A.reset(base_mark)
        ya_t = A.alloc([16, 512], BF16)
        yb_t = A.alloc([8, 512], BF16)
        gat = [A.alloc([4, 512], BF16) for _ in range(2)]
        gbt = [A.alloc([4, 512], BF16) for _ in range(2)]
        mxs = [A.alloc([16, 512], BF16)] * 2
        ring = Ring(6)
        t1 = [A.alloc([512], F32) for _ in range(2)]
        t2 = [A.alloc([512], F32) for _ in range(2)]
        xt = [A.alloc([D], F32) for _ in range(4)]
        oe = [A.alloc([D], F32) for _ in range(4)]
        junk = A.alloc([D], BF16)
        gpost_t = A.alloc([D], F32)
        ssO = [A.alloc([16], F32) for _ in range(2)]
        m1 = A.alloc([4], F32)
        tmo = A.alloc([4], F32)
        rso = A.alloc([4], F32)
        s.dma("sp", gpost_t, gpost_bc, writes=["gpost"], key="c3")
        wa_v = wa_t
        wb_v = wb_t
        wo_v = wo_t
        ya_v = YA_s.rearrange("(kc p) t -> p kc t", p=128)
        yb_v = YB_s.rearrange("(kc p) t -> p kc t", p=128)
        cnt = {"pair": 0, "mm": 0, "t": 0, "g": 0, "x": 0}

        def chain_a(tt, sso):
            for j in range(4):
                s.op("dve", lambda e, j=j: e.tensor_reduce(out=m1[:, j:j + 1], in_=sso[:, j * 4:(j + 1) * 4], axis=mybir.AxisListType.X, op=ALU.add),
                     reads=[("ssO", tt % 2, j, cb) for cb in range(4)], writes=[("m1", j)])
                s.op("dve", lambda e, j=j: e.tensor_scalar(out=tmo[:, j:j + 1], in0=m1[:, j:j + 1], scalar1=1.0 / D, scalar2=EPS4, op0=ALU.mult, op1=ALU.add),
                     reads=[("m1", j)], writes=[("tmo", j)])
                s.op("pool", lambda e, j=j: e.tensor_tensor(out=rso[:, j:j + 1], in0=tmo[:, j:j + 1], in1=mhalf[:, 0:1], op=ALU.pow),
                     reads=[("tmo", j), "mhalf"], writes=[("rso", j)])

        def chain_b(tt):
            t0 = tt * 512
            for j in range(4):
                s.op("dve", lambda e, j=j: e.scalar_tensor_tensor(out=oe[j], in0=oe[j], scalar=rso[:, j:j + 1], in1=gpost_t, op0=ALU.mult, op1=ALU.mult),
                     reads=[("oe", j), ("rso", j), "gpost"], writes=[("oe", j)])
                s.op("pool", lambda e, j=j: e.tensor_tensor(out=xt[j], in0=oe[j], in1=xt[j], op=ALU.add),
                     reads=[("oe", j), ("xt", j)], writes=[("xt", j)])
                s.dma("pool", X1_s[t0 + j * 128:t0 + (j + 1) * 128, :], xt[j], reads=[("xt", j)], writes=[], key="x1s%d" % j)

        for tt in range(S // 512):
            t0 = tt * 512
            mx = mxs[tt % 2]
            sso = ssO[tt % 2]
            s.dma("sp", ya_t, ya_v[:, :, t0:t0 + 512], writes=["ya_t"], key="ya")
            s.dma("sp", yb_t, yb_v[:, :, t0:t0 + 512], writes=["yb_t"], key="yb_t")
            for grp in range(4):
                ia = ring.load(wa_v, 16, grp * 512, "ra")
                ib = ring.load(wb_v, 8, grp * 512, "ra")
                gi = cnt["g"] % 2
                cnt["g"] += 1
                s.dma("sp", gat[gi], GA_s[grp * 512:(grp + 1) * 512, t0:t0 + 512].rearrange("(c p) t -> p c t", p=128), writes=[("gat", gi)], key="ga%d" % gi)
                s.dma("sp", gbt[gi], GB_s[grp * 512:(grp + 1) * 512, t0:t0 + 512].rearrange("(c p) t -> p c t", p=128), writes=[("gbt", gi)], key="gb%d" % gi)
                for c4 in range(4):
                    n = grp * 4 + c4
                    pr = cnt["pair"] % 2
                    cnt["pair"] += 1
                    bA, bB = 2 * pr, 2 * pr + 1

                    def ab(e, ia=ia, ib=ib, c4=c4, bA=bA, bB=bB):
                        ins = None
                        for kc in range(16):
                            e.matmul(ps(bA), lhsT=ring.w(ia, kc)[:, c4 * 128:(c4 + 1) * 128], rhs=ya_t[:, kc, :], start=(kc == 0), stop=(kc == 15))
                        for kc in range(8):
                            ins = e.matmul(ps(bB), lhsT=ring.w(ib, kc)[:, c4 * 128:(c4 + 1) * 128], rhs=yb_t[:, kc, :], start=(kc == 0), stop=(kc == 7))
                        return ins

                    s.op("pe", ab, reads=ring.res(ia) + ring.res(ib) + ["ya_t", "yb_t"], writes=[("ps", bA), ("ps", bB)])
                    ti = cnt["t"] % 2
                    cnt["t"] += 1
                    s.op("dve", lambda e, ti=ti, bA=bA, gi=gi, c4=c4: e.tensor_tensor(out=t1[ti], in0=ps(bA), in1=gat[gi][:, c4, :], op=ALU.mult),
                         reads=[("ps", bA), ("gat", gi)], writes=[("t1", ti)])
                    s.op("dve", lambda e, ti=ti, bB=bB, gi=gi, c4=c4: e.tensor_tensor(out=t2[ti], in0=ps(bB), in1=gbt[gi][:, c4, :], op=ALU.mult),
                         reads=[("ps", bB), ("gbt", gi)], writes=[("t2", ti)])
                    s.op("pool", lambda e, ti=ti, n=n, mx=mx: e.tensor_tensor(out=mx[:, n, :], in0=t1[ti], in1=t2[ti], op=ALU.add),
                         reads=[("t1", ti), ("t2", ti)], writes=[("mx", 0)])
                if grp == 0 and tt > 0:
                    chain_b(tt - 1)
            for j in range(4):
                s.dma("sp", xt[j], x[t0 + j * 128:t0 + (j + 1) * 128, :], writes=[("xt", j)], key="px%d" % j)
            for cb in range(4):
                io = ring.load(wo_v, 16, cb * 512, "ra")
                for j in range(4):
                    bank = 4 + cnt["mm"] % 4
                    cnt["mm"] += 1

                    def om(e, io=io, j=j, bank=bank, mx=mx):
                        ins = None
                        for kc in range(16):
                            ins = e.matmul(ps(bank), lhsT=mx[:, kc, j * 128:(j + 1) * 128], rhs=ring.w(io, kc), start=(kc == 0), stop=(kc == 15))
                        return ins

                    s.op("pe", om, reads=ring.res(io) + [("mx", 0)], writes=[("ps", bank)])
                    s.op("dve", lambda e, bank=bank, j=j, cb=cb: e.tensor_copy(out=oe[j][:, cb * 512:(cb + 1) * 512], in_=ps(bank)),
                         reads=[("ps", bank)], writes=[("oe", j)])
                    s.op("act", lambda e, j=j, cb=cb, sso=sso: e.activation(out=junk[:, 0:512], in_=oe[j][:, cb * 512:(cb + 1) * 512], func=AF.Square,
                                                                      accum_out=sso[:, j * 4 + cb:j * 4 + cb + 1]),
                         reads=[("oe", j)], writes=["junk", ("ssO", tt % 2, j, cb)])
            if debug and tt == 0:
                dbg_o = nc.dram_tensor("dbg_o", [128, D], F32, kind="ExternalOutput").ap()
                dbg_mx = nc.dram_tensor("dbg_mx", [128, 16 * 512], BF16, kind="ExternalOutput").ap()
                dbg_ss = nc.dram_tensor("dbg_ss", [128, 16], F32, kind="ExternalOutput").ap()
                s.dma("sp", dbg_o, oe[0], reads=[("oe", 0)], writes=[], key="dbg0")
                s.dma("sp", dbg_mx, mx.rearrange("p a b -> p (a b)"), reads=[("mx", 0)], writes=[], key="dbg1")
                s.dma("sp", dbg_ss, sso, reads=[("ssO", 0, jj, cc_) for jj in range(4) for cc_ in range(4)], writes=[], key="dbg2")
            chain_a(tt, sso)
        chain_b(S // 512 - 1)
        s.fence()

    def ph5():
        A.reset(base_mark)
        x1Ts = [A.alloc([16, 512], BF16) for _ in range(2)]
        pTs = [A.alloc([2, 512], BF16) for _ in range(2)]
        xp = [A.alloc([D], F32) for _ in range(2)]
        xq = [A.alloc([D], F32) for _ in range(4)]
        x1bs = [A.alloc([D], BF16) for _ in range(2)]
        oe = [A.alloc([D], F32) for _ in range(4)]
        ring = Ring(6)
        wP = A.alloc([2, D], BF16)
        sgt = [A.alloc([512], F32) for _ in range(2)]
        junk = A.alloc([D], BF16)
        gple_t = A.alloc([D], F32)
        ptf = [A.alloc([256], F32) for _ in range(2)]
        ptb = [A.alloc([256], BF16) for _ in range(2)]
        ss2 = A.alloc([4], F32)
        tm2 = A.alloc([4], F32)
        rs2 = A.alloc([4], F32)
        s.dma("sp", gple_t, gple_bc, writes=["gple"], key="c4")
        s.dma("sp", wP, wple_bf.rearrange("(kc kp) n -> kp kc n", kp=128), reads=["W"], writes=["wP"], key="c5")
        wg_v = wg_t
        cnt = {"mm": 0, "sg": 0, "x": 0, "q": 0, "tr": 0, "p": 0}

        def prep(tt):
            t0 = tt * 512
            x1T = x1Ts[tt % 2]
            pT = pTs[tt % 2]
            for j in range(4):
                xi = cnt["x"] % 2
                cnt["x"] += 1
                s.dma("sp", xp[xi], X1_s[t0 + j * 128:t0 + (j + 1) * 128, :], writes=[("xp", xi)], key="xp%d" % xi)
                s.op("act", lambda e, xi=xi: e.activation(out=x1bs[xi], in_=xp[xi], func=AF.Copy), reads=[("xp", xi)], writes=[("x1b", xi)])
                for half in range(2):
                    bank = 6 + cnt["tr"] % 2
                    cnt["tr"] += 1

                    def trf(e, half=half, xi=xi, bank=bank):
                        ins = None
                        pb = psbf(bank)
                        for k in range(8):
                            kc = half * 8 + k
                            ins = e.transpose(pb[:, k * 128:(k + 1) * 128], x1bs[xi][:, kc * 128:(kc + 1) * 128], idt)
                        return ins

                    s.op("pe", trf, reads=[("x1b", xi), "idt"], writes=[("ps", bank)])
                    s.op("dve", lambda e, half=half, j=j, bank=bank, x1T=x1T: e.tensor_copy(out=x1T[:, half * 8:(half + 1) * 8, j * 128:(j + 1) * 128],
                                                                                   in_=psbf(bank).rearrange("p (k t) -> p k t", k=8)),
                         reads=[("ps", bank)], writes=[("x1T", tt % 2)])
                pi = cnt["p"] % 2
                cnt["p"] += 1
                s.dma("sp", ptf[pi], p[t0 + j * 128:t0 + (j + 1) * 128, :], writes=[("ptf", pi)], key="pp%d" % pi)
                s.op("act", lambda e, pi=pi: e.activation(out=ptb[pi], in_=ptf[pi], func=AF.Copy), reads=[("ptf", pi)], writes=[("ptb", pi)])
                bank = 6 + cnt["tr"] % 2
                cnt["tr"] += 1

                def trp(e, pi=pi, bank=bank):
                    pb = psbf(bank)
                    e.transpose(pb[:, 0:128], ptb[pi][:, 0:128], idt)
                    return e.transpose(pb[:, 128:256], ptb[pi][:, 128:256], idt)

                s.op("pe", trp, reads=[("ptb", pi), "idt"], writes=[("ps", bank)])
                s.op("dve", lambda e, j=j, bank=bank, pT=pT: e.tensor_copy(out=pT[:, :, j * 128:(j + 1) * 128], in_=psbf(bank)[:, 0:256].rearrange("p (k t) -> p k t", k=2)),
                     reads=[("ps", bank)], writes=[("pT", tt % 2)])

        def gate(tt):
            t0 = tt * 512
            x1T = x1Ts[tt % 2]
            pT = pTs[tt % 2]
            for cb in range(4):
                ig = ring.load(wg_v, 16, cb * 512, "rb")
                if cb == 2:
                    for j in range(4):
                        s.dma("sp", xq[j], X1_s[t0 + j * 128:t0 + (j + 1) * 128, :], writes=[("xq", j)], key="xq%d" % j)
                for j in range(4):
                    bank = cnt["mm"] % 6
                    cnt["mm"] += 1
                    bank2 = cnt["mm"] % 6
                    cnt["mm"] += 1

                    def gm(e, ig=ig, j=j, bank=bank, x1T=x1T):
                        ins = None
                        for kc in range(16):
                            ins = e.matmul(ps(bank), lhsT=x1T[:, kc, j * 128:(j + 1) * 128], rhs=ring.w(ig, kc), start=(kc == 0), stop=(kc == 15))
                        return ins

                    def pm(e, j=j, cb=cb, bank2=bank2, pT=pT):
                        e.matmul(ps(bank2), lhsT=pT[:, 0, j * 128:(j + 1) * 128], rhs=wP[:, 0, cb * 512:(cb + 1) * 512], start=True, stop=False)
                        return e.matmul(ps(bank2), lhsT=pT[:, 1, j * 128:(j + 1) * 128], rhs=wP[:, 1, cb * 512:(cb + 1) * 512], start=False, stop=True)

                    s.op("pe", gm, reads=ring.res(ig) + [("x1T", tt % 2)], writes=[("ps", bank)])
                    s.op("pe", pm, reads=[("pT", tt % 2), "wP"], writes=[("ps", bank2)])
                    si = cnt["sg"] % 2
                    cnt["sg"] += 1
                    s.op("act", lambda e, si=si, bank=bank: e.activation(out=sgt[si], in_=ps(bank), func=AF.Sigmoid),
                         reads=[("ps", bank)], writes=[("sgt", si)])
                    s.op("dve", lambda e, si=si, bank2=bank2, j=j, cb=cb: e.tensor_tensor(out=oe[j][:, cb * 512:(cb + 1) * 512], in0=ps(bank2), in1=sgt[si], op=ALU.mult),
                         reads=[("ps", bank2), ("sgt", si)], writes=[("oe", j)])
        def gate_final(tt):
            t0 = tt * 512
            for j in range(4):
                s.op("act", lambda e, j=j: e.activation(out=junk, in_=oe[j], func=AF.Square, accum_out=ss2[:, j:j + 1]),
                     reads=[("oe", j)], writes=["junk", ("ss2", j)])
                s.op("dve", lambda e, j=j: e.tensor_scalar(out=tm2[:, j:j + 1], in0=ss2[:, j:j + 1], scalar1=1.0 / D, scalar2=EPS, op0=ALU.mult, op1=ALU.add),
                     reads=[("ss2", j)], writes=[("tm2", j)])
                s.op("pool", lambda e, j=j: e.tensor_tensor(out=rs2[:, j:j + 1], in0=tm2[:, j:j + 1], in1=mhalf[:, 0:1], op=ALU.pow),
                     reads=[("tm2", j), "mhalf"], writes=[("rs2", j)])
            for j in range(4):
                qi = j
                s.op("dve", lambda e, j=j: e.scalar_tensor_tensor(out=oe[j], in0=oe[j], scalar=rs2[:, j:j + 1], in1=gple_t, op0=ALU.mult, op1=ALU.mult),
                     reads=[("oe", j), ("rs2", j), "gple"], writes=[("oe", j)])
                s.op("pool", lambda e, j=j, qi=qi: e.tensor_tensor(out=xq[qi], in0=oe[j], in1=xq[qi], op=ALU.add),
                     reads=[("oe", j), ("xq", qi)], writes=[("xq", qi)])
                s.dma("pool", y[t0 + j * 128:t0 + (j + 1) * 128, :], xq[qi], reads=[("xq", qi)], writes=[], key="yo%d" % qi)

        NT = S // 512
        prep(0)
        prep(1)
        for tt in range(NT):
            gate(tt)
            if tt + 2 < NT:
                prep(tt + 2)
            gate_final(tt)
        s.fence()


    for _n, _f in enumerate((ph0, ph1, ph2, ph3, ph4, ph5)):
        if _n in phases:
            _f()
    s.finish()
    return nc


def _tables():
    bf = ml_dtypes.bfloat16
    ident = np.eye(128, dtype=np.float32).astype(bf)
    a = np.arange(64)
    ang = 2.0 * np.pi * np.outer(a, a) / 64.0
    f64t = np.concatenate([np.cos(ang), -np.sin(ang)], axis=1).astype(np.float32).astype(bf)
    b = np.arange(128, dtype=np.int64)[:, None, None]
    ap = np.arange(64, dtype=np.int64)[None, :, None]
    bp = np.arange(128, dtype=np.int64)[None, None, :]
    k = ap + 64 * bp
    ph = 2.0 * np.pi * ((b * k) % S).astype(np.float64) / S
    cosG = np.cos(ph).reshape(128, S).astype(np.float32).astype(bf)
    sinG = np.sin(ph).reshape(128, S).astype(np.float32).astype(bf)
    nsinG = (-np.sin(ph)).reshape(128, S).astype(np.float32).astype(bf)
    c = np.arange(128)
    angc = 2.0 * np.pi * np.outer(c, c) / 128.0
    scale = 1.0 / 1024.0
    ccs = np.concatenate([np.cos(angc) * scale, np.sin(angc) * scale], axis=1).astype(np.float32).astype(bf)
    return dict(ident=ident, f64t=f64t, cosG=cosG, sinG=sinG, nsinG=nsinG, ccs=ccs)


def _shared_inputs(inp):
    f = np.float32

    def fm(v):
        return np.ascontiguousarray(np.asarray(v, f).reshape(16, 128).T)

    cw = np.asarray(inp["conv_w"], f)[0]
    cols = [fm(cw[k]) for k in range(4)]
    cols.append(fm(np.asarray(inp["conv_b"], f)[0]))
    for name in ("b_rgate", "b_igate", "lam"):
        v = np.asarray(inp[name], f)[0]
        cols += [fm(v[0]), fm(v[1])]
    vecs = np.ascontiguousarray(np.concatenate(cols, axis=1))
    assert vecs.shape == (128, NV)

    def bc(v):
        return np.ascontiguousarray(np.broadcast_to(np.asarray(v, f).reshape(1, D), (128, D)))

    sh = dict(
        w_in=np.ascontiguousarray(np.asarray(inp["w_in"], f)[0]),
        w_a=np.ascontiguousarray(np.asarray(inp["w_a"], f)[0]),
        w_b=np.ascontiguousarray(np.asarray(inp["w_b"], f)[0]),
        w_out=np.ascontiguousarray(np.asarray(inp["w_out"], f)[0]),
        w_gate=np.ascontiguousarray(np.asarray(inp["w_ple_gate"], f)[0]),
        w_ple=np.ascontiguousarray(np.asarray(inp["w_ple"], f)[0]),
        w_rg=np.ascontiguousarray(np.asarray(inp["w_rgate"], f)[0].reshape(4096, 128)),
        w_ig=np.ascontiguousarray(np.asarray(inp["w_igate"], f)[0].reshape(4096, 128)),
        vecs=vecs,
        gpre_bc=bc(inp["g_pre"][0]),
        gpost_bc=bc(inp["g_post"][0]),
        gple_bc=bc(inp["g_ple"][0]),
    )
    sh.update(_tables())
    return sh


def kernel(**inp):
    f = np.float32
    xp = np.asarray(inp["x_prompt"], f)
    xs = np.asarray(inp["x_sample"], f)
    pp = np.asarray(inp["p_prompt"], f)[0]
    psm = np.asarray(inp["p_sample"], f)[0]
    seqs = [(xp[i], pp[i]) for i in range(4)] + [(xs[0], psm[0])]
    sh = _shared_inputs(inp)
    in_maps = []
    for c in range(8):
        xi, pi = seqs[c] if c < 5 else seqs[c - 5]
        m = dict(sh)
        m["x"] = np.ascontiguousarray(xi)
        m["p"] = np.ascontiguousarray(pi)
        in_maps.append(m)
    nc = build_program()
    res = run_bass_kernel_spmd(nc, in_maps, core_ids=list(range(8)))
    outs = [np.asarray(res.results[c]["y"], f).reshape(S, D) for c in range(5)]
    y_prompt = np.stack(outs[:4], axis=0)
    y_sample = outs[4][None]
    return (y_prompt, y_sample)
```

```python
from contextlib import ExitStack

import numpy as np
import ml_dtypes
import concourse.bass as bass
import concourse.mybir as mybir
from concourse.bass_utils import run_bass_kernel_spmd

F32 = mybir.dt.float32
BF16 = mybir.dt.bfloat16
AF = mybir.ActivationFunctionType
ALU = mybir.AluOpType


class Sched:
    ENGS = ("pe", "act", "dve", "pool", "sp")

    def __init__(self, nc):
        self.nc = nc
        self.es = ExitStack()
        self.ops = {e: [] for e in self.ENGS}
        self.cnt = {e: 0 for e in self.ENGS}
        self.dcnt = {}
        self.last_w = {}
        self.readers = {}
        self.waited = {e: {} for e in self.ENGS}
        self.sems = {}

    def sbuf(self, name, shape, dt):
        return self.es.enter_context(self.nc.sbuf_tensor(name, shape, dt))

    def psum(self, name, shape, dt):
        return self.es.enter_context(self.nc.psum_tensor(name, shape, dt))

    def _deps(self, eng, reads, writes):
        need = {}

        def add(k, v):
            if k == "pe" and eng == "pe":
                return
            if need.get(k, 0) < v:
                need[k] = v

        for r in reads:
            t = self.last_w.get(r)
            if t:
                add(*t)
        for w in writes:
            t = self.last_w.get(w)
            if t:
                add(*t)
            for k, v in self.readers.get(w, {}).items():
                add(k, v)
        waits = []
        wd = self.waited[eng]
        for k, v in need.items():
            if wd.get(k, 0) >= v:
                continue
            wd[k] = v
            waits.append((k, v))
        return waits

    def _commit(self, tok, reads, writes):
        k, v = tok
        for r in reads:
            d = self.readers.setdefault(r, {})
            if d.get(k, 0) < v:
                d[k] = v
        for w in writes:
            self.last_w[w] = tok
            self.readers[w] = {}

    def op(self, eng, fn, reads=(), writes=()):
        waits = self._deps(eng, reads, writes)
        self.cnt[eng] += 1
        tok = (eng, self.cnt[eng])
        self.ops[eng].append((waits, fn, True))
        self._commit(tok, reads, writes)

    def dma(self, q, out, in_, reads=(), writes=(), key=None, **kw):
        pairs = out if isinstance(out, list) else [(out, in_)]
        waits = self._deps(q, reads, writes)
        dk = "d:" + key
        self.dcnt[dk] = self.dcnt.get(dk, 0) + 16 * len(pairs)
        tok = (dk, self.dcnt[dk])

        def fn(e, pairs=pairs, dk=dk, kw=kw):
            for o, i in pairs:
                e.dma_start(out=o, in_=i, **kw).then_inc(self.sems[dk], 16)

        self.ops[q].append((waits, fn, False))
        self._commit(tok, reads, writes)

    def fence(self):
        allw = [(e, self.cnt[e]) for e in self.ENGS if self.cnt[e] > 0] + list(self.dcnt.items())
        for eng in self.ENGS:
            waits = []
            wd = self.waited[eng]
            for k, v in allw:
                if k == eng:
                    continue
                if wd.get(k, 0) >= v:
                    continue
                wd[k] = v
                waits.append((k, v))
            if waits:
                self.ops[eng].append((waits, None, False))
        self.last_w = {}
        self.readers = {}

    def finish(self):
        nc = self.nc
        waits = [(k, v) for k, v in self.dcnt.items()] + [(e, self.cnt[e]) for e in self.ENGS if self.cnt[e] > 0 and e != "sp"]
        self.ops["sp"].append((waits, None, False))
        for e in self.ENGS:
            self.sems[e] = self.es.enter_context(nc.semaphore("sem_" + e))
        for i, dk in enumerate(self.dcnt):
            self.sems[dk] = self.es.enter_context(nc.semaphore("semd_%d" % i))

        def emit(name):
            def f(e):
                for waits, fn, inc in self.ops[name]:
                    for k, v in waits:
                        e.wait_ge(self.sems[k], v)
                    if fn is None:
                        continue
                    ins = fn(e)
                    if inc:
                        ins.then_inc(self.sems[name], 1)
            return f

        with nc.Block() as block:
            block.tensor(emit("pe"))
            block.scalar(emit("act"))
            block.vector(emit("dve"))
            block.gpsimd(emit("pool"))
            block.sync(emit("sp"))
        self.es.close()


S = 8192
D = 2048
NIN = 10240
EPS = 1e-6
NV = 176
V_CW, V_CB, V_BR, V_BI, V_LAM = 0, 64, 80, 112, 144


class Arena:
    def __init__(self, s, nbytes):
        self.t = s.sbuf("arena", [128, nbytes // 2], BF16)
        self.cap = nbytes // 2
        self.off = 0

    def mark(self):
        return self.off

    def reset(self, m):
        self.off = m

    def alloc(self, shape, dt):
        n = 1
        for d in shape:
            n *= d
        nel = n * 2 if dt == F32 else n
        nel = (nel + 15) // 16 * 16
        assert self.off + nel <= self.cap, ("arena overflow", self.off, nel, self.cap)
        ap = self.t[:, self.off:self.off + nel]
        self.off += nel
        if dt == F32:
            ap = ap.bitcast(F32)
        ap = ap[:, 0:n]
        if len(shape) == 2:
            ap = ap.rearrange("p (a b) -> p a b", a=shape[0])
        elif len(shape) == 3:
            ap = ap.rearrange("p (a b c) -> p a b c", a=shape[0], b=shape[1])
        elif len(shape) == 4:
            ap = ap.rearrange("p (a b c d) -> p a b c d", a=shape[0], b=shape[1], c=shape[2])
        return ap


def build_program(debug=False, phases=(0, 1, 2, 3, 4, 5), p3="AdaBC", p3g=8):
    nc = bass.Bass("TRN2", target_bir_lowering=False)
    ikind = "ExternalInput"
    skind = "ExternalOutput" if debug else "Internal"

    def din(name, shape, dt=F32):
        return nc.dram_tensor(name, shape, dt, kind=ikind).ap()

    def dscr(name, shape, dt=BF16):
        return nc.dram_tensor(name, shape, dt, kind=skind).ap()

    x = din("x", [S, D])
    p = din("p", [S, 256])
    w_in = din("w_in", [D, NIN])
    w_a = din("w_a", [D, D])
    w_b = din("w_b", [1024, D])
    w_out = din("w_out", [D, D])
    w_gate = din("w_gate", [D, D])
    w_ple = din("w_ple", [256, D])
    w_rg = din("w_rg", [4096, 128])
    w_ig = din("w_ig", [4096, 128])
    vecs = din("vecs", [128, NV])
    gpre_bc = din("gpre_bc", [128, D])
    gpost_bc = din("gpost_bc", [128, D])
    gple_bc = din("gple_bc", [128, D])
    ident = din("ident", [128, 128], BF16)
    f64t = din("f64t", [64, 128], BF16)
    cosG = din("cosG", [128, S], BF16)
    sinG = din("sinG", [128, S], BF16)
    nsinG = din("nsinG", [128, S], BF16)
    ccs = din("ccs", [128, 256], BF16)
    y = nc.dram_tensor("y", [S, D], F32, kind="ExternalOutput").ap()

    win_bf = dscr("win_bf", [D, NIN])
    wa_bf = dscr("wa_bf", [D, D])
    wb_bf = dscr("wb_bf", [1024, D])
    wout_bf = dscr("wout_bf", [D, D])
    wgate_bf = dscr("wgate_bf", [D, D])
    wple_bf = dscr("wple_bf", [256, D])
    wrg_bf = dscr("wrg_bf", [4096, 128])
    wig_bf = dscr("wig_bf", [4096, 128])
    XA_s = dscr("XA_s", [D, S])
    ZA_s = dscr("ZA_s", [D, S])
    XB_s = dscr("XB_s", [8, S, 128])
    ZB_s = dscr("ZB_s", [1024, S])
    GA_s = dscr("GA_s", [D, S])
    GB_s = dscr("GB_s", [D, S])
    YA_s = dscr("YA_s", [D, S])
    YB_s = dscr("YB_s", [1024, S])
    YA_t = YA_s.rearrange("a b -> (a b)").rearrange("(tt p kc t) -> tt p kc t", p=128, kc=16, t=512)
    YB_t = YB_s.rearrange("a b -> (a b)").rearrange("(tt p kc t) -> tt p kc t", p=128, kc=8, t=512)

    s = Sched(nc)
    A = Arena(s, 204 * 1024)
    PS = [s.psum("psb%d" % i, [128, 512], F32) for i in range(8)]

    def ps(i):
        return PS[i][:]

    def psbf(i):
        return PS[i][:].bitcast(BF16)

    vec_t = A.alloc([NV], F32)
    idt = A.alloc([128], BF16)
    hbr = A.alloc([32], F32)
    hbi = A.alloc([32], F32)
    sc4 = A.alloc([32], F32)
    mhalf = A.alloc([4], F32)
    s.dma("sp", vec_t, vecs, writes=["vec"], key="c0")
    s.dma("sp", idt, ident, writes=["idt"], key="c1")
    s.op("dve", lambda e: e.memset(mhalf, -0.5), writes=["mhalf"])
    s.op("dve", lambda e: e.tensor_scalar(out=hbr, in0=vec_t[:, V_BR:V_BR + 32], scalar1=0.5, scalar2=None, op0=ALU.mult),
         reads=["vec"], writes=["hbr"])
    s.op("dve", lambda e: e.tensor_scalar(out=hbi, in0=vec_t[:, V_BI:V_BI + 32], scalar1=0.5, scalar2=None, op0=ALU.mult),
         reads=["vec"], writes=["hbi"])
    s.op("act", lambda e: e.activation(out=sc4, in_=vec_t[:, V_LAM:V_LAM + 32], func=AF.Exp, scale=-1.0), reads=["vec"], writes=["sc4"])
    s.op("act", lambda e: e.activation(out=sc4, in_=sc4, func=AF.Ln, bias=1.0), reads=["sc4"], writes=["sc4"])
    s.op("dve", lambda e: e.tensor_scalar(out=sc4, in0=sc4, scalar1=-4.0, scalar2=None, op0=ALU.mult), reads=["sc4"], writes=["sc4"])
    base_mark = A.mark()

    def tiled(t, nkb):
        return t.rearrange("a b -> (a b)").rearrange("(g kb kp k8 n) -> g kb kp k8 n", kb=nkb, kp=128, k8=8, n=512)

    wa_t, wb_t, wo_t, wg_t = tiled(wa_bf, 2), tiled(wb_bf, 1), tiled(wout_bf, 2), tiled(wgate_bf, 2)

    def ph0():
        def cast(dst, src, rows, cols):
            cw = min(cols, 2048)
            nseg = cols // cw
            rstep = 128 if nseg > 1 else 512
            pairs = []
            for r0 in range(0, rows, rstep):
                r1 = min(rows, r0 + rstep)
                if nseg > 1:
                    o = dst[r0:r1, :].rearrange("r (g c) -> r g c", g=nseg)
                    i = src[r0:r1, :].rearrange("r (g c) -> r g c", g=nseg)
                else:
                    o = dst[r0:r1, :]
                    i = src[r0:r1, :]
                pairs.append((o, i))
            s.dma("pool", pairs, None, writes=["W"], key="cast")

        cast(win_bf, w_in, D, NIN)
        def cast_tiled(dst_t, src, nkb):
            sv = src.rearrange("(kb k8 kp) (g n) -> g kb kp k8 n", k8=8, kp=128, n=512)
            pairs = [(dst_t[g, kb], sv[g, kb]) for g in range(4) for kb in range(nkb)]
            s.dma("pool", pairs, None, writes=["W"], key="cast")

        cast_tiled(wa_t, w_a, 2)
        cast_tiled(wb_t, w_b, 1)
        cast_tiled(wo_t, w_out, 2)
        cast_tiled(wg_t, w_gate, 2)
        cast(wple_bf, w_ple, 256, D)
        cast(wrg_bf, w_rg, 4096, 128)
        cast(wig_bf, w_ig, 4096, 128)
        s.fence()

    def ph1():
        A.reset(base_mark)
        gpre_t = A.alloc([D], F32)
        xts = [A.alloc([D], F32) for _ in range(3)]
        hbs = [A.alloc([D], BF16) for _ in range(2)]
        junk = A.alloc([D], BF16)
        TT1 = 1024
        NSUB = TT1 // 128
        hTs = [A.alloc([16, TT1], BF16) for _ in range(2)]
        wts = [A.alloc([16, 512], BF16) for _ in range(3)]
        stg = [A.alloc([4, TT1], BF16) for _ in range(4)]
        sst = [A.alloc([NSUB], F32) for _ in range(2)]
        tmt = [A.alloc([NSUB], F32) for _ in range(2)]
        rst = [A.alloc([NSUB], F32) for _ in range(2)]
        s.dma("sp", gpre_t, gpre_bc, writes=["gpre"], key="c2")
        win_v = win_bf.rearrange("(kc kp) n -> kp kc n", kp=128)
        cnt = {"x": 0, "w": 0, "stg": 0, "mm": 0, "tr": 0}
        order = [(g, "xa") for g in range(0, 4)] + [(8, "xb"), (9, "xb")] + [(g, "za") for g in range(4, 8)] + \
                [(10, "zb"), (11, "zb")] + [(g, "ga") for g in range(12, 16)] + [(g, "gb") for g in range(16, 20)]
        dst_fm = {"xa": (XA_s, 0), "za": (ZA_s, 4), "zb": (ZB_s, 10), "ga": (GA_s, 12), "gb": (GB_s, 16)}
        funcs = {"xa": None, "xb": None, "za": AF.Silu, "zb": AF.Silu, "ga": AF.Sigmoid, "gb": AF.Sigmoid}

        def prep(tt):
            t0 = tt * TT1
            hT = hTs[tt % 2]
            ss, tm, rs = sst[tt % 2], tmt[tt % 2], rst[tt % 2]
            for j in range(NSUB):
                xi = cnt["x"] % 3
                hi = cnt["x"] % 2
                cnt["x"] += 1
                xt, hb = xts[xi], hbs[hi]
                s.dma("sp", xt, x[t0 + j * 128:t0 + (j + 1) * 128, :], writes=[("xt", xi)], key="x%d" % xi)
                s.op("act", lambda e, xt=xt, ss=ss, j=j: e.activation(out=junk, in_=xt, func=AF.Square, accum_out=ss[:, j:j + 1]),
                     reads=[("xt", xi)], writes=["junk", ("ss", tt % 2, j)])
                s.op("dve", lambda e, ss=ss, tm=tm, j=j: e.tensor_scalar(out=tm[:, j:j + 1], in0=ss[:, j:j + 1], scalar1=1.0 / D, scalar2=EPS,
                                                                     op0=ALU.mult, op1=ALU.add),
                     reads=[("ss", tt % 2, j)], writes=[("tm", tt % 2, j)])
                s.op("pool", lambda e, rs=rs, tm=tm, j=j: e.tensor_tensor(out=rs[:, j:j + 1], in0=tm[:, j:j + 1], in1=mhalf[:, 0:1], op=ALU.pow),
                     reads=[("tm", tt % 2, j), "mhalf"], writes=[("rs", tt % 2, j)])
                s.op("dve", lambda e, xt=xt, hb=hb, rs=rs, j=j: e.scalar_tensor_tensor(out=hb, in0=xt, scalar=rs[:, j:j + 1], in1=gpre_t,
                                                                                    op0=ALU.mult, op1=ALU.mult),
                     reads=[("xt", xi), ("rs", tt % 2, j), "gpre"], writes=[("hb", hi)])
                for half in range(2):
                    bank = 6 + cnt["tr"] % 2
                    cnt["tr"] += 1

                    def trf(e, hb=hb, half=half, bank=bank):
                        ins = None
                        pb = psbf(bank)
                        for k in range(8):
                            kc = half * 8 + k
                            ins = e.transpose(pb[:, k * 128:(k + 1) * 128], hb[:, kc * 128:(kc + 1) * 128], idt)
                        return ins

                    s.op("pe", trf, reads=[("hb", hi), "idt"], writes=[("ps", bank)])
                    s.op("dve", lambda e, hT=hT, half=half, bank=bank, j=j: e.tensor_copy(
                        out=hT[:, half * 8:(half + 1) * 8, j * 128:(j + 1) * 128],
                        in_=psbf(bank).rearrange("p (k t) -> p k t", k=8)),
                        reads=[("ps", bank)], writes=[("hT", tt % 2)])

        def groups(tt, lo, hi_):
            t0 = tt * TT1
            hT = hTs[tt % 2]
            for (g, kind) in order[lo:hi_]:
                wi = cnt["w"] % 3
                cnt["w"] += 1
                wt = wts[wi]
                s.dma("sp", wt, win_v[:, :, g * 512:(g + 1) * 512], reads=["W"], writes=[("wt", wi)], key="w%d" % wi)
                si = cnt["stg"] % 4
                if kind != "xb":
                    cnt["stg"] += 1
                    st = stg[si]
                    for c4 in range(4):
                        pr = cnt["mm"] % 3
                        cnt["mm"] += 1
                        bks = (2 * pr, 2 * pr + 1)

                        def mm(e, wt=wt, hT=hT, c4=c4, bks=bks):
                            ins = None
                            for kc in range(16):
                                for th in range(2):
                                    ins = e.matmul(ps(bks[th]), lhsT=wt[:, kc, c4 * 128:(c4 + 1) * 128], rhs=hT[:, kc, th * 512:(th + 1) * 512],
                                                   start=(kc == 0), stop=(kc == 15))
                            return ins

                        s.op("pe", mm, reads=[("wt", wi), ("hT", tt % 2)], writes=[("ps", bks[0]), ("ps", bks[1])])
                        fn = funcs[kind]
                        for th in range(2):
                            bank = bks[th]
                            if fn is None:
                                s.op("dve", lambda e, st=st, c4=c4, bank=bank, th=th: e.tensor_copy(out=st[:, c4, th * 512:(th + 1) * 512], in_=ps(bank)),
                                     reads=[("ps", bank)], writes=[("stg", si)])
                            else:
                                s.op("act", lambda e, st=st, c4=c4, bank=bank, fn=fn, th=th: e.activation(out=st[:, c4, th * 512:(th + 1) * 512], in_=ps(bank), func=fn),
                                     reads=[("ps", bank)], writes=[("stg", si)])
                    dst, g0 = dst_fm[kind]
                    c0 = (g - g0) * 512
                    s.dma("pool", dst[c0:c0 + 512, t0:t0 + TT1].rearrange("(j p) t -> p j t", p=128), st,
                          reads=[("stg", si)], writes=[], key="st%d" % si)
                else:
                    cb = g - 8
                    for j in range(NSUB):
                        si = cnt["stg"] % 4
                        cnt["stg"] += 1
                        st = stg[si]
                        bank = 2 * (cnt["mm"] % 3) + (j % 2)
                        if j % 2 == 1:
                            cnt["mm"] += 1

                        def mm(e, wt=wt, hT=hT, j=j, bank=bank):
                            ins = None
                            for kc in range(16):
                                ins = e.matmul(ps(bank), lhsT=hT[:, kc, j * 128:(j + 1) * 128], rhs=wt[:, kc, :],
                                               start=(kc == 0), stop=(kc == 15))
                            return ins

                        s.op("pe", mm, reads=[("wt", wi), ("hT", tt % 2)], writes=[("ps", bank)])
                        s.op("dve", lambda e, st=st, bank=bank: e.tensor_copy(out=st[:, 0, 0:512], in_=ps(bank)),
                             reads=[("ps", bank)], writes=[("stg", si)])
                        s.dma("pool", XB_s[cb * 4:(cb + 1) * 4, t0 + j * 128:t0 + (j + 1) * 128, :].rearrange("g p c -> p g c"),
                              st[:, 0, 0:512].rearrange("p (b c) -> p b c", c=128), reads=[("stg", si)], writes=[], key="st%d" % si)

        NT = S // TT1
        prep(0)
        for tt in range(NT):
            groups(tt, 0, 10)
            if tt + 1 < NT:
                prep(tt + 1)
            groups(tt, 10, 20)
        s.fence()

    def ph2():
        A.reset(base_mark)
        xa_pad = A.alloc([S + 4], BF16)
        zsq = [A.alloc([2048], BF16) for _ in range(2)]
        c2 = [A.alloc([S], BF16) for _ in range(2)]
        hf = A.alloc([S], F32)
        NWS = 3
        wsA = [A.alloc([2048], F32) for _ in range(NWS)]
        wsS = [A.alloc([2048], F32) for _ in range(NWS)]
        wsU = [A.alloc([2048], F32) for _ in range(NWS)]
        hbq = [A.alloc([2048], F32) for _ in range(2)]
        yst = [A.alloc([2048], BF16) for _ in range(2)]
        gw = [A.alloc([4, 128], BF16) for _ in range(2)]
        dg = [A.alloc([4, 128], BF16) for _ in range(2)]
        s.op("dve", lambda e: e.memset(xa_pad[:, 0:2], 0.0), writes=["xa"])
        s.op("dve", lambda e: e.memset(xa_pad[:, S + 2:S + 4], 0.0), writes=["xa"])
        cnt = {"u": 0, "gs": 0, "cv": 0, "ev": 0}
        units = []

        def pro_load(h):
            hb2 = h % 2
            s.dma("sp", xa_pad[:, 2:S + 2], XA_s[h * 128:(h + 1) * 128, :], writes=["xa"], key="xa")
            pairs = []
            for d in range(2):
                pairs.append((gw[hb2][:, d * 2 + 0, :], wrg_bf[(d * 16 + h) * 128:(d * 16 + h + 1) * 128, :]))
                pairs.append((gw[hb2][:, d * 2 + 1, :], wig_bf[(d * 16 + h) * 128:(d * 16 + h + 1) * 128, :]))
            s.dma("sp", pairs, None, writes=[("gw", hb2)], key="gw%d" % hb2)
            for k in range(4):
                s.op("dve", lambda e, k=k: e.tensor_scalar(out=dg[hb2][:, k, :], in0=idt, scalar1=vec_t[:, V_CW + k * 16 + h:V_CW + k * 16 + h + 1],
                                                      scalar2=None, op0=ALU.mult),
                     reads=["idt", "vec"], writes=[("dg", hb2)])

        def pro_conv(h, chs=range(16)):
            hb2 = h % 2
            cbuf = c2[hb2]
            for ch in chs:
                bank = cnt["cv"] % 2
                cnt["cv"] += 1

                def cv(e, ch=ch, bank=bank):
                    ins = None
                    for k in range(4):
                        ins = e.matmul(ps(bank), lhsT=dg[hb2][:, k, :], rhs=xa_pad[:, ch * 512 + k:ch * 512 + k + 512], start=(k == 0), stop=(k == 3))
                    return ins

                s.op("pe", cv, reads=[("dg", hb2), "xa"], writes=[("ps", bank)])
                if True:
                    s.op("dve", lambda e, ch=ch, bank=bank: e.tensor_scalar(out=cbuf[:, ch * 512:(ch + 1) * 512], in0=ps(bank),
                                                                     scalar1=vec_t[:, V_CB + h:V_CB + h + 1], scalar2=None, op0=ALU.add),
                         reads=[("ps", bank), "vec"], writes=[("c2", hb2, ch // 4)])
                else:
                    s.op("act", lambda e, ch=ch, bank=bank: e.activation(out=cbuf[:, ch * 512:(ch + 1) * 512], in_=ps(bank), func=AF.Identity,
                                                                  bias=vec_t[:, V_CB + h:V_CB + h + 1]),
                         reads=[("ps", bank), "vec"], writes=[("c2", hb2, ch // 4)])

        def make_unit(h, d, q, u):
            hb2 = h % 2
            cbuf = c2[hb2]
            vi = d * 16 + h
            ws = u % NWS
            Aw, Sw, Uw = wsA[ws], wsS[ws], wsU[ws]

            def s1():
                for cc in range(4):
                    slot = cnt["gs"] % 3
                    cnt["gs"] += 1
                    b0, b1 = 2 + slot * 2, 3 + slot * 2
                    tok0 = q * 2048 + cc * 512

                    def gm(e, b0=b0, b1=b1, tok0=tok0):
                        e.matmul(ps(b0), lhsT=gw[hb2][:, d * 2 + 0, :], rhs=cbuf[:, tok0:tok0 + 512], start=True, stop=True)
                        return e.matmul(ps(b1), lhsT=gw[hb2][:, d * 2 + 1, :], rhs=cbuf[:, tok0:tok0 + 512], start=True, stop=True)

                    s.op("pe", gm, reads=[("gw", hb2), ("c2", hb2, q)], writes=[("ps", b0), ("ps", b1)])
                    s.op("act", lambda e, cc=cc, b0=b0: e.activation(out=Aw[:, cc * 512:(cc + 1) * 512], in_=ps(b0), func=AF.Tanh,
                                                               scale=0.5, bias=hbr[:, vi:vi + 1]),
                         reads=[("ps", b0), "hbr"], writes=[("A", ws)])
                    s.op("act", lambda e, cc=cc, b1=b1: e.activation(out=Uw[:, cc * 512:(cc + 1) * 512], in_=ps(b1), func=AF.Tanh,
                                                               scale=0.5, bias=hbi[:, vi:vi + 1]),
                         reads=[("ps", b1), "hbi"], writes=[("U", ws)])
                s.op("act", lambda e: e.activation(out=Aw, in_=Aw, func=AF.Exp, scale=sc4[:, vi:vi + 1], bias=sc4[:, vi:vi + 1]),
                     reads=[("A", ws), "sc4"], writes=[("A", ws)])
                s.op("act", lambda e: e.activation(out=Sw, in_=Aw, func=AF.Square), reads=[("A", ws)], writes=[("S", ws)])
                s.op("dve", lambda e: e.scalar_tensor_tensor(out=Uw, in0=Uw, scalar=1.0, in1=cbuf[:, q * 2048:(q + 1) * 2048],
                                                             op0=ALU.add, op1=ALU.mult),
                     reads=[("U", ws), ("c2", hb2, q)], writes=[("U", ws)])

            def s2():
                s.op("act", lambda e: e.activation(out=Sw, in_=Sw, func=AF.Sqrt, scale=-1.0, bias=1.0), reads=[("S", ws)], writes=[("S", ws)])
                s.op("dve", lambda e: e.scalar_tensor_tensor(out=Uw, in0=Uw, scalar=0.5, in1=Sw, op0=ALU.mult, op1=ALU.mult),
                     reads=[("U", ws), ("S", ws)], writes=[("U", ws)])
                if d == 0:
                    init = 0.0 if q == 0 else hf[:, q * 2048 - 1:q * 2048]
                    rd = [("A", ws), ("U", ws)] + ([] if q == 0 else [("hf", q - 1)])
                    s.op("dve", lambda e: e.tensor_tensor_scan(out=hf[:, q * 2048:(q + 1) * 2048], data0=Aw, data1=Uw,
                                                               initial=init, op0=ALU.mult, op1=ALU.add),
                         reads=rd, writes=[("hf", q)])
                else:
                    hi = u % 2
                    hq = hbq[hi]
                    init = 0.0 if q == 3 else hbq[1 - hi][:, 0:1]
                    rd = [("A", ws), ("U", ws)] + ([] if q == 3 else [("hbq", 1 - hi)])
                    s.op("dve", lambda e: e.tensor_tensor_scan(out=hq[:, ::-1], data0=Aw[:, ::-1], data1=Uw[:, ::-1],
                                                               initial=init, op0=ALU.mult, op1=ALU.add),
                         reads=rd, writes=[("hbq", hi)])
                    zi_ = u % 2
                    s.dma("sp", zsq[zi_], ZA_s[h * 128:(h + 1) * 128, q * 2048:(q + 1) * 2048], writes=[("zsq", zi_)], key="zs%d" % zi_)
                    s.op("pool", lambda e: e.tensor_tensor(out=Uw, in0=hq, in1=hf[:, q * 2048:(q + 1) * 2048], op=ALU.add),
                         reads=[("hbq", hi), ("hf", q)], writes=[("U", ws)])
                    s.op("pool", lambda e: e.tensor_tensor(out=yst[zi_], in0=Uw, in1=zsq[zi_], op=ALU.mult),
                         reads=[("U", ws), ("zsq", zi_)], writes=[("yst", zi_)])
                    s.dma("pool", YA_t[q * 4:(q + 1) * 4, :, h, :].rearrange("tt p t -> p tt t"), yst[zi_].rearrange("p (tt t) -> p tt t", tt=4),
                          reads=[("yst", zi_)], writes=[], key="ys%d" % zi_)

            return s1, s2

        for h in range(16):
            first = True
            for d in range(2):
                qs = [0, 1, 2, 3] if d == 0 else [3, 2, 1, 0]
                for q in qs:
                    u = cnt["u"]
                    cnt["u"] += 1
                    s1, s2 = make_unit(h, d, q, u)
                    units.append((h if first else None, s1, s2))
                    first = False
        CONV_SPLIT = {1: [0, 1, 2], 2: [3, 4, 5], 3: [6, 7], 4: [8, 9], 5: [10, 11], 6: [12, 13], 7: [14, 15]}
        pro_load(0)
        pro_conv(0)
        pro_load(1)
        for i, (hp, s1, s2) in enumerate(units):
            hh, idx = i // 8, i % 8
            s1()
            if idx >= 1 and hh + 1 < 16:
                pro_conv(hh + 1, CONV_SPLIT[idx])
            if idx == 7 and hh + 2 < 16:
                pro_load(hh + 2)
            if i >= 1:
                units[i - 1][2]()
        units[-1][2]()
        s.fence()

    def ph3():
        A.reset(base_mark)
        Xg = A.alloc([128, 128], BF16)
        Zre = A.alloc([64, 128], BF16)
        Zim = A.alloc([64, 128], BF16)
        cG = A.alloc([64, 128], BF16)
        sG = A.alloc([64, 128], BF16)
        nG = A.alloc([64, 128], BF16)
        ccs_t = A.alloc([256], BF16)
        f64_t = A.alloc([128], BF16)
        Yt = [A.alloc([4, 2, 128], BF16) for _ in range(2)]
        szb = A.alloc([S], BF16)
        ybt = A.alloc([S], BF16)
        s.dma("sp", cG.rearrange("p a b -> p (a b)"), cosG, writes=["cG"], key="t0")
        s.dma("sp", sG.rearrange("p a b -> p (a b)"), sinG, writes=["sG"], key="t1")
        s.dma("sp", nG.rearrange("p a b -> p (a b)"), nsinG, writes=["nG"], key="t2")
        s.dma("sp", ccs_t, ccs, writes=["ccs"], key="t3")
        s.dma("sp", f64_t[0:64, :], f64t, writes=["f64"], key="t4")
        cnt = {"a": 0, "b": 0, "c": 0, "ev": 0}
        for g in range(p3g):
            s.dma("sp", Xg[0:64].rearrange("p b c -> p (b c)"), XB_s[g].rearrange("(a b) c -> a (b c)", a=64), writes=["Xg"], key="xg")
            s.dma("sp", szb, ZB_s[g * 128:(g + 1) * 128, :], writes=["szb"], key="szb")
            for c4 in range(32 if "A" in p3 else 0):
                bank = cnt["a"] % 2
                cnt["a"] += 1

                def sa(e, c4=c4, bank=bank):
                    ins = None
                    for cc in range(4):
                        c = c4 * 4 + cc
                        ins = e.matmul(ps(bank)[:, cc * 128:(cc + 1) * 128], lhsT=Xg[0:64, :, c], rhs=f64_t[0:64, :], start=True, stop=True)
                    return ins

                s.op("pe", sa, reads=["Xg", "f64"], writes=[("ps", bank)])
                pv = ps(bank).rearrange("p (cc ri a) -> p ri a cc", cc=4, ri=2, a=64)
                if "d" in p3:
                    s.op("dve", lambda e, c4=c4, pv=pv: e.tensor_copy(out=Zre[:, :, c4 * 4:(c4 + 1) * 4], in_=pv[:, 0]),
                         reads=[("ps", bank)], writes=["Zre"])
                if "a" in p3:
                    s.op("dve", lambda e, c4=c4, pv=pv: e.tensor_copy(out=Zim[:, :, c4 * 4:(c4 + 1) * 4], in_=pv[:, 1]),
                         reads=[("ps", bank)], writes=["Zim"])
            for a2 in range(32 if "B" in p3 else 0):
                bank = 2 + cnt["b"] % 4
                cnt["b"] += 1

                def sb(e, a2=a2, bank=bank):
                    ins = None
                    for ai in range(2):
                        ap_ = 2 * a2 + ai
                        o = ai * 256
                        e.matmul(ps(bank)[:, o:o + 128], lhsT=Zre[:, ap_, :], rhs=cG[:, ap_, :], start=True, stop=False)
                        e.matmul(ps(bank)[:, o:o + 128], lhsT=Zim[:, ap_, :], rhs=sG[:, ap_, :], start=False, stop=True)
                        e.matmul(ps(bank)[:, o + 128:o + 256], lhsT=Zre[:, ap_, :], rhs=nG[:, ap_, :], start=True, stop=False)
                        ins = e.matmul(ps(bank)[:, o + 128:o + 256], lhsT=Zim[:, ap_, :], rhs=cG[:, ap_, :], start=False, stop=True)
                    return ins

                s.op("pe", sb, reads=["Zre", "Zim", "cG", "sG", "nG"], writes=[("ps", bank)])
                a4 = a2 // 2
                yi = a4 % 2
                Y = Yt[yi]
                half = a2 % 2
                if a2 % 2 == 0:
                    s.op("dve", lambda e, Y=Y, bank=bank, half=half: e.tensor_copy(out=Y[:, half * 2:half * 2 + 2].rearrange("p a r b -> p (a r b)"), in_=ps(bank)),
                         reads=[("ps", bank)], writes=[("Y", yi)])
                else:
                    s.op("act", lambda e, Y=Y, bank=bank, half=half: e.activation(out=Y[:, half * 2:half * 2 + 2].rearrange("p a r b -> p (a r b)"), in_=ps(bank), func=AF.Copy),
                         reads=[("ps", bank)], writes=[("Y", yi)])
                if a2 % 2 == 1 and "C" in p3:
                    cbank = 6 + cnt["c"] % 2
                    cnt["c"] += 1

                    def sc(e, Y=Y, cbank=cbank):
                        ins = None
                        for al in range(4):
                            e.matmul(ps(cbank)[:, al * 128:(al + 1) * 128], lhsT=ccs_t[:, 0:128], rhs=Y[:, al, 0, :], start=True, stop=False)
                            ins = e.matmul(ps(cbank)[:, al * 128:(al + 1) * 128], lhsT=ccs_t[:, 128:256], rhs=Y[:, al, 1, :], start=False, stop=True)
                        return ins

                    s.op("pe", sc, reads=[("Y", yi), "ccs"], writes=[("ps", cbank)])
                    zv = szb.rearrange("p (b a) -> p a b", a=64)[:, a4 * 4:a4 * 4 + 4, :]
                    ov = ybt.rearrange("p (b a) -> p a b", a=64)[:, a4 * 4:a4 * 4 + 4, :]
                    s.op("dve", lambda e, zv=zv, ov=ov, cbank=cbank: e.tensor_tensor(out=ov, in0=ps(cbank).rearrange("p (a b) -> p a b", a=4), in1=zv, op=ALU.mult),
                         reads=[("ps", cbank), "szb"], writes=["ybt"])
            s.dma("pool", YB_t[:, :, g, :].rearrange("tt p t -> p tt t"), ybt.rearrange("p (tt t) -> p tt t", tt=16), reads=["ybt"], writes=[], key="yb")
        s.fence()

    X1_s = dscr("X1_s", [S, D], F32)

    class Ring:
        def __init__(self, nslots):
            self.slots = [A.alloc([8, 512], BF16) for _ in range(nslots)]
            self.n = nslots
            self.c = 0

        def load(self, view, nk, c0, tag):
            ids = []
            for b in range(nk // 8):
                i = self.c % self.n
                self.c += 1
                s.dma("sp", self.slots[i], view[c0 // 512, b], reads=["W"], writes=[("ws", i)], key="%s%d" % (tag, i))
                ids.append(i)
            return ids

        def w(self, ids, kc):
            return self.slots[ids[kc // 8]][:, kc % 8, :]

        def res(self, ids):
            return [("ws", i) for i in ids]

    EPS4 = EPS
    def ph4():
        A.reset(base_mark)
        ya_t = A.alloc([16, 512], BF16)
        yb_t = A.alloc([8, 512], BF16)
        gat = [A.alloc([4, 512], BF16) for _ in range(2)]
        gbt = [A.alloc([4, 512], BF16) for _ in range(2)]
        mxs = [A.alloc([16, 512], BF16)] * 2
        ring = Ring(6)
        t1 = [A.alloc([512], F32) for _ in range(2)]
        t2 = [A.alloc([512], F32) for _ in range(2)]
        xt = [A.alloc([D], F32) for _ in range(4)]
        oe = [A.alloc([D], F32) for _ in range(4)]
        junk = A.alloc([D], BF16)
        gpost_t = A.alloc([D], F32)
        ssO = [A.alloc([16], F32) for _ in range(2)]
        m1 = A.alloc([4], F32)
        tmo = A.alloc([4], F32)
        rso = A.alloc([4], F32)
        s.dma("sp", gpost_t, gpost_bc, writes=["gpost"], key="c3")
        wa_v = wa_t
        wb_v = wb_t
        wo_v = wo_t
        ya_v = YA_s.rearrange("(kc p) t -> p kc t", p=128)
        yb_v = YB_s.rearrange("(kc p) t -> p kc t", p=128)
        cnt = {"pair": 0, "mm": 0, "t": 0, "g": 0, "x": 0}

        def chain_a(tt, sso):
            for j in range(4):
                s.op("dve", lambda e, j=j: e.tensor_reduce(out=m1[:, j:j + 1], in_=sso[:, j * 4:(j + 1) * 4], axis=mybir.AxisListType.X, op=ALU.add),
                     reads=[("ssO", tt % 2, j, cb) for cb in range(4)], writes=[("m1", j)])
                s.op("dve", lambda e, j=j: e.tensor_scalar(out=tmo[:, j:j + 1], in0=m1[:, j:j + 1], scalar1=1.0 / D, scalar2=EPS4, op0=ALU.mult, op1=ALU.add),
                     reads=[("m1", j)], writes=[("tmo", j)])
                s.op("pool", lambda e, j=j: e.tensor_tensor(out=rso[:, j:j + 1], in0=tmo[:, j:j + 1], in1=mhalf[:, 0:1], op=ALU.pow),
                     reads=[("tmo", j), "mhalf"], writes=[("rso", j)])

        def chain_b(tt):
            t0 = tt * 512
            for j in range(4):
                s.op("dve", lambda e, j=j: e.scalar_tensor_tensor(out=oe[j], in0=oe[j], scalar=rso[:, j:j + 1], in1=gpost_t, op0=ALU.mult, op1=ALU.mult),
                     reads=[("oe", j), ("rso", j), "gpost"], writes=[("oe", j)])
                s.op("pool", lambda e, j=j: e.tensor_tensor(out=xt[j], in0=oe[j], in1=xt[j], op=ALU.add),
                     reads=[("oe", j), ("xt", j)], writes=[("xt", j)])
                s.dma("pool", X1_s[t0 + j * 128:t0 + (j + 1) * 128, :], xt[j], reads=[("xt", j)], writes=[], key="x1s%d" % j)

        for tt in range(S // 512):
            t0 = tt * 512
            mx = mxs[tt % 2]
            sso = ssO[tt % 2]
            s.dma("sp", ya_t, YA_t[tt], writes=["ya_t"], key="ya")
            s.dma("sp", yb_t, YB_t[tt], writes=["yb_t"], key="yb_t")
            for grp in range(4):
                ia = ring.load(wa_v, 16, grp * 512, "ra")
                ib = ring.load(wb_v, 8, grp * 512, "ra")
                gi = cnt["g"] % 2
                cnt["g"] += 1
                s.dma("sp", gat[gi], GA_s[grp * 512:(grp + 1) * 512, t0:t0 + 512].rearrange("(c p) t -> p c t", p=128), writes=[("gat", gi)], key="ga%d" % gi)
                s.dma("sp", gbt[gi], GB_s[grp * 512:(grp + 1) * 512, t0:t0 + 512].rearrange("(c p) t -> p c t", p=128), writes=[("gbt", gi)], key="gb%d" % gi)
                for c4 in range(4):
                    n = grp * 4 + c4
                    pr = cnt["pair"] % 2
                    cnt["pair"] += 1
                    bA, bB = 2 * pr, 2 * pr + 1

                    def ab(e, ia=ia, ib=ib, c4=c4, bA=bA, bB=bB):
                        ins = None
                        for kc in range(16):
                            e.matmul(ps(bA), lhsT=ring.w(ia, kc)[:, c4 * 128:(c4 + 1) * 128], rhs=ya_t[:, kc, :], start=(kc == 0), stop=(kc == 15))
                        for kc in range(8):
                            ins = e.matmul(ps(bB), lhsT=ring.w(ib, kc)[:, c4 * 128:(c4 + 1) * 128], rhs=yb_t[:, kc, :], start=(kc == 0), stop=(kc == 7))
                        return ins

                    s.op("pe", ab, reads=ring.res(ia) + ring.res(ib) + ["ya_t", "yb_t"], writes=[("ps", bA), ("ps", bB)])
                    ti = cnt["t"] % 2
                    cnt["t"] += 1
                    s.op("dve", lambda e, ti=ti, bA=bA, gi=gi, c4=c4: e.tensor_tensor(out=t1[ti], in0=ps(bA), in1=gat[gi][:, c4, :], op=ALU.mult),
                         reads=[("ps", bA), ("gat", gi)], writes=[("t1", ti)])
                    s.op("dve", lambda e, ti=ti, bB=bB, gi=gi, c4=c4: e.tensor_tensor(out=t2[ti], in0=ps(bB), in1=gbt[gi][:, c4, :], op=ALU.mult),
                         reads=[("ps", bB), ("gbt", gi)], writes=[("t2", ti)])
                    s.op("pool", lambda e, ti=ti, n=n, mx=mx: e.tensor_tensor(out=mx[:, n, :], in0=t1[ti], in1=t2[ti], op=ALU.add),
                         reads=[("t1", ti), ("t2", ti)], writes=[("mx", 0)])
                if grp == 0 and tt > 0:
                    chain_b(tt - 1)
            for j in range(4):
                s.dma("sp", xt[j], x[t0 + j * 128:t0 + (j + 1) * 128, :], writes=[("xt", j)], key="px%d" % j)
            for cb in range(4):
                io = ring.load(wo_v, 16, cb * 512, "ra")
                for j in range(4):
                    bank = 4 + cnt["mm"] % 4
                    cnt["mm"] += 1

                    def om(e, io=io, j=j, bank=bank, mx=mx):
                        ins = None
                        for kc in range(16):
                            ins = e.matmul(ps(bank), lhsT=mx[:, kc, j * 128:(j + 1) * 128], rhs=ring.w(io, kc), start=(kc == 0), stop=(kc == 15))
                        return ins

                    s.op("pe", om, reads=ring.res(io) + [("mx", 0)], writes=[("ps", bank)])
                    s.op("dve", lambda e, bank=bank, j=j, cb=cb: e.tensor_copy(out=oe[j][:, cb * 512:(cb + 1) * 512], in_=ps(bank)),
                         reads=[("ps", bank)], writes=[("oe", j)])
                    s.op("act", lambda e, j=j, cb=cb, sso=sso: e.activation(out=junk[:, 0:512], in_=oe[j][:, cb * 512:(cb + 1) * 512], func=AF.Square,
                                                                      accum_out=sso[:, j * 4 + cb:j * 4 + cb + 1]),
                         reads=[("oe", j)], writes=["junk", ("ssO", tt % 2, j, cb)])
            if debug and tt == 0:
                dbg_o = nc.dram_tensor("dbg_o", [128, D], F32, kind="ExternalOutput").ap()
                dbg_mx = nc.dram_tensor("dbg_mx", [128, 16 * 512], BF16, kind="ExternalOutput").ap()
                dbg_ss = nc.dram_tensor("dbg_ss", [128, 16], F32, kind="ExternalOutput").ap()
                s.dma("sp", dbg_o, oe[0], reads=[("oe", 0)], writes=[], key="dbg0")
                s.dma("sp", dbg_mx, mx.rearrange("p a b -> p (a b)"), reads=[("mx", 0)], writes=[], key="dbg1")
                s.dma("sp", dbg_ss, sso, reads=[("ssO", 0, jj, cc_) for jj in range(4) for cc_ in range(4)], writes=[], key="dbg2")
            chain_a(tt, sso)
        chain_b(S // 512 - 1)
        s.fence()

    def ph5():
        A.reset(base_mark)
        x1Ts = [A.alloc([16, 512], BF16) for _ in range(2)]
        pTs = [A.alloc([2, 512], BF16) for _ in range(2)]
        xp = [A.alloc([D], F32) for _ in range(2)]
        xq = [A.alloc([D], F32) for _ in range(4)]
        x1bs = [A.alloc([D], BF16) for _ in range(2)]
        oe = [A.alloc([D], F32) for _ in range(4)]
        ring = Ring(6)
        wP = A.alloc([2, D], BF16)
        sgt = [A.alloc([512], F32) for _ in range(2)]
        junk = A.alloc([D], BF16)
        gple_t = A.alloc([D], F32)
        ptf = [A.alloc([256], F32) for _ in range(2)]
        ptb = [A.alloc([256], BF16) for _ in range(2)]
        ss2 = A.alloc([4], F32)
        tm2 = A.alloc([4], F32)
        rs2 = A.alloc([4], F32)
        s.dma("sp", gple_t, gple_bc, writes=["gple"], key="c4")
        s.dma("sp", wP, wple_bf.rearrange("(kc kp) n -> kp kc n", kp=128), reads=["W"], writes=["wP"], key="c5")
        wg_v = wg_t
        cnt = {"mm": 0, "sg": 0, "x": 0, "q": 0, "tr": 0, "p": 0}

        def prep(tt):
            t0 = tt * 512
            x1T = x1Ts[tt % 2]
            pT = pTs[tt % 2]
            for j in range(4):
                xi = cnt["x"] % 2
                cnt["x"] += 1
                s.dma("sp", xp[xi], X1_s[t0 + j * 128:t0 + (j + 1) * 128, :], writes=[("xp", xi)], key="xp%d" % xi)
                s.op("act", lambda e, xi=xi: e.activation(out=x1bs[xi], in_=xp[xi], func=AF.Copy), reads=[("xp", xi)], writes=[("x1b", xi)])
                for half in range(2):
                    bank = 6 + cnt["tr"] % 2
                    cnt["tr"] += 1

                    def trf(e, half=half, xi=xi, bank=bank):
                        ins = None
                        pb = psbf(bank)
                        for k in range(8):
                            kc = half * 8 + k
                            ins = e.transpose(pb[:, k * 128:(k + 1) * 128], x1bs[xi][:, kc * 128:(kc + 1) * 128], idt)
                        return ins

                    s.op("pe", trf, reads=[("x1b", xi), "idt"], writes=[("ps", bank)])
                    s.op("dve", lambda e, half=half, j=j, bank=bank, x1T=x1T: e.tensor_copy(out=x1T[:, half * 8:(half + 1) * 8, j * 128:(j + 1) * 128],
                                                                                   in_=psbf(bank).rearrange("p (k t) -> p k t", k=8)),
                         reads=[("ps", bank)], writes=[("x1T", tt % 2)])
                pi = cnt["p"] % 2
                cnt["p"] += 1
                s.dma("sp", ptf[pi], p[t0 + j * 128:t0 + (j + 1) * 128, :], writes=[("ptf", pi)], key="pp%d" % pi)
                s.op("act", lambda e, pi=pi: e.activation(out=ptb[pi], in_=ptf[pi], func=AF.Copy), reads=[("ptf", pi)], writes=[("ptb", pi)])
                bank = 6 + cnt["tr"] % 2
                cnt["tr"] += 1

                def trp(e, pi=pi, bank=bank):
                    pb = psbf(bank)
                    e.transpose(pb[:, 0:128], ptb[pi][:, 0:128], idt)
                    return e.transpose(pb[:, 128:256], ptb[pi][:, 128:256], idt)

                s.op("pe", trp, reads=[("ptb", pi), "idt"], writes=[("ps", bank)])
                s.op("dve", lambda e, j=j, bank=bank, pT=pT: e.tensor_copy(out=pT[:, :, j * 128:(j + 1) * 128], in_=psbf(bank)[:, 0:256].rearrange("p (k t) -> p k t", k=2)),
                     reads=[("ps", bank)], writes=[("pT", tt % 2)])

        def gate(tt):
            t0 = tt * 512
            x1T = x1Ts[tt % 2]
            pT = pTs[tt % 2]
            for cb in range(4):
                ig = ring.load(wg_v, 16, cb * 512, "rb")
                if cb == 2:
                    for j in range(4):
                        s.dma("sp", xq[j], X1_s[t0 + j * 128:t0 + (j + 1) * 128, :], writes=[("xq", j)], key="xq%d" % j)
                for j in range(4):
                    bank = cnt["mm"] % 6
                    cnt["mm"] += 1
                    bank2 = cnt["mm"] % 6
                    cnt["mm"] += 1

                    def gm(e, ig=ig, j=j, bank=bank, x1T=x1T):
                        ins = None
                        for kc in range(16):
                            ins = e.matmul(ps(bank), lhsT=x1T[:, kc, j * 128:(j + 1) * 128], rhs=ring.w(ig, kc), start=(kc == 0), stop=(kc == 15))
                        return ins

                    def pm(e, j=j, cb=cb, bank2=bank2, pT=pT):
                        e.matmul(ps(bank2), lhsT=pT[:, 0, j * 128:(j + 1) * 128], rhs=wP[:, 0, cb * 512:(cb + 1) * 512], start=True, stop=False)
                        return e.matmul(ps(bank2), lhsT=pT[:, 1, j * 128:(j + 1) * 128], rhs=wP[:, 1, cb * 512:(cb + 1) * 512], start=False, stop=True)

                    s.op("pe", gm, reads=ring.res(ig) + [("x1T", tt % 2)], writes=[("ps", bank)])
                    s.op("pe", pm, reads=[("pT", tt % 2), "wP"], writes=[("ps", bank2)])
                    si = cnt["sg"] % 2
                    cnt["sg"] += 1
                    s.op("act", lambda e, si=si, bank=bank: e.activation(out=sgt[si], in_=ps(bank), func=AF.Sigmoid),
                         reads=[("ps", bank)], writes=[("sgt", si)])
                    s.op("dve", lambda e, si=si, bank2=bank2, j=j, cb=cb: e.tensor_tensor(out=oe[j][:, cb * 512:(cb + 1) * 512], in0=ps(bank2), in1=sgt[si], op=ALU.mult),
                         reads=[("ps", bank2), ("sgt", si)], writes=[("oe", j)])
        def gate_final(tt):
            t0 = tt * 512
            for j in range(4):
                s.op("act", lambda e, j=j: e.activation(out=junk, in_=oe[j], func=AF.Square, accum_out=ss2[:, j:j + 1]),
                     reads=[("oe", j)], writes=["junk", ("ss2", j)])
                s.op("dve", lambda e, j=j: e.tensor_scalar(out=tm2[:, j:j + 1], in0=ss2[:, j:j + 1], scalar1=1.0 / D, scalar2=EPS, op0=ALU.mult, op1=ALU.add),
                     reads=[("ss2", j)], writes=[("tm2", j)])
                s.op("pool", lambda e, j=j: e.tensor_tensor(out=rs2[:, j:j + 1], in0=tm2[:, j:j + 1], in1=mhalf[:, 0:1], op=ALU.pow),
                     reads=[("tm2", j), "mhalf"], writes=[("rs2", j)])
            for j in range(4):
                qi = j
                s.op("dve", lambda e, j=j: e.scalar_tensor_tensor(out=oe[j], in0=oe[j], scalar=rs2[:, j:j + 1], in1=gple_t, op0=ALU.mult, op1=ALU.mult),
                     reads=[("oe", j), ("rs2", j), "gple"], writes=[("oe", j)])
                s.op("pool", lambda e, j=j, qi=qi: e.tensor_tensor(out=xq[qi], in0=oe[j], in1=xq[qi], op=ALU.add),
                     reads=[("oe", j), ("xq", qi)], writes=[("xq", qi)])
                s.dma("pool", y[t0 + j * 128:t0 + (j + 1) * 128, :], xq[qi], reads=[("xq", qi)], writes=[], key="yo%d" % qi)

        NT = S // 512
        prep(0)
        prep(1)
        for tt in range(NT):
            gate(tt)
            if tt + 2 < NT:
                prep(tt + 2)
            gate_final(tt)
        s.fence()


    for _n, _f in enumerate((ph0, ph1, ph2, ph3, ph4, ph5)):
        if _n in phases:
            _f()
    s.finish()
    return nc


def _tables():
    bf = ml_dtypes.bfloat16
    ident = np.eye(128, dtype=np.float32).astype(bf)
    a = np.arange(64)
    ang = 2.0 * np.pi * np.outer(a, a) / 64.0
    f64t = np.concatenate([np.cos(ang), -np.sin(ang)], axis=1).astype(np.float32).astype(bf)
    b = np.arange(128, dtype=np.int64)[:, None, None]
    ap = np.arange(64, dtype=np.int64)[None, :, None]
    bp = np.arange(128, dtype=np.int64)[None, None, :]
    k = ap + 64 * bp
    ph = 2.0 * np.pi * ((b * k) % S).astype(np.float64) / S
    cosG = np.cos(ph).reshape(128, S).astype(np.float32).astype(bf)
    sinG = np.sin(ph).reshape(128, S).astype(np.float32).astype(bf)
    nsinG = (-np.sin(ph)).reshape(128, S).astype(np.float32).astype(bf)
    c = np.arange(128)
    angc = 2.0 * np.pi * np.outer(c, c) / 128.0
    scale = 1.0 / 1024.0
    ccs = np.concatenate([np.cos(angc) * scale, np.sin(angc) * scale], axis=1).astype(np.float32).astype(bf)
    return dict(ident=ident, f64t=f64t, cosG=cosG, sinG=sinG, nsinG=nsinG, ccs=ccs)


def _shared_inputs(inp):
    f = np.float32

    def fm(v):
        return np.ascontiguousarray(np.asarray(v, f).reshape(16, 128).T)

    cw = np.asarray(inp["conv_w"], f)[0]
    cols = [fm(cw[k]) for k in range(4)]
    cols.append(fm(np.asarray(inp["conv_b"], f)[0]))
    for name in ("b_rgate", "b_igate", "lam"):
        v = np.asarray(inp[name], f)[0]
        cols += [fm(v[0]), fm(v[1])]
    vecs = np.ascontiguousarray(np.concatenate(cols, axis=1))
    assert vecs.shape == (128, NV)

    def bc(v):
        return np.ascontiguousarray(np.broadcast_to(np.asarray(v, f).reshape(1, D), (128, D)))

    sh = dict(
        w_in=np.ascontiguousarray(np.asarray(inp["w_in"], f)[0]),
        w_a=np.ascontiguousarray(np.asarray(inp["w_a"], f)[0]),
        w_b=np.ascontiguousarray(np.asarray(inp["w_b"], f)[0]),
        w_out=np.ascontiguousarray(np.asarray(inp["w_out"], f)[0]),
        w_gate=np.ascontiguousarray(np.asarray(inp["w_ple_gate"], f)[0]),
        w_ple=np.ascontiguousarray(np.asarray(inp["w_ple"], f)[0]),
        w_rg=np.ascontiguousarray(np.asarray(inp["w_rgate"], f)[0].reshape(4096, 128)),
        w_ig=np.ascontiguousarray(np.asarray(inp["w_igate"], f)[0].reshape(4096, 128)),
        vecs=vecs,
        gpre_bc=bc(inp["g_pre"][0]),
        gpost_bc=bc(inp["g_post"][0]),
        gple_bc=bc(inp["g_ple"][0]),
    )
    sh.update(_tables())
    return sh


def kernel(**inp):
    f = np.float32
    xp = np.asarray(inp["x_prompt"], f)
    xs = np.asarray(inp["x_sample"], f)
    pp = np.asarray(inp["p_prompt"], f)[0]
    psm = np.asarray(inp["p_sample"], f)[0]
    seqs = [(xp[i], pp[i]) for i in range(4)] + [(xs[0], psm[0])]
    sh = _shared_inputs(inp)
    in_maps = []
    for c in range(8):
        xi, pi = seqs[c] if c < 5 else seqs[c - 5]
        m = dict(sh)
        m["x"] = np.ascontiguousarray(xi)
        m["p"] = np.ascontiguousarray(pi)
        in_maps.append(m)
    nc = build_program()
    res = run_bass_kernel_spmd(nc, in_maps, core_ids=list(range(8)))
    outs = [np.asarray(res.results[c]["y"], f).reshape(S, D) for c in range(5)]
    y_prompt = np.stack(outs[:4], axis=0)
    y_sample = outs[4][None]
    return (y_prompt, y_sample)
```

```python
from contextlib import ExitStack

import numpy as np
import ml_dtypes
import concourse.bass as bass
import concourse.mybir as mybir
from concourse.bass_utils import run_bass_kernel_spmd

F32 = mybir.dt.float32
BF16 = mybir.dt.bfloat16
AF = mybir.ActivationFunctionType
ALU = mybir.AluOpType


class Sched:
    ENGS = ("pe", "act", "dve", "pool", "sp")

    def __init__(self, nc):
        self.nc = nc
        self.es = ExitStack()
        self.ops = {e: [] for e in self.ENGS}
        self.cnt = {e: 0 for e in self.ENGS}
        self.dcnt = {}
        self.last_w = {}
        self.readers = {}
        self.waited = {e: {} for e in self.ENGS}
        self.sems = {}

    def sbuf(self, name, shape, dt):
        return self.es.enter_context(self.nc.sbuf_tensor(name, shape, dt))

    def psum(self, name, shape, dt):
        return self.es.enter_context(self.nc.psum_tensor(name, shape, dt))

    def _deps(self, eng, reads, writes):
        need = {}

        def add(k, v):
            if k == "pe" and eng == "pe":
                return
            if need.get(k, 0) < v:
                need[k] = v

        for r in reads:
            t = self.last_w.get(r)
            if t:
                add(*t)
        for w in writes:
            t = self.last_w.get(w)
            if t:
                add(*t)
            for k, v in self.readers.get(w, {}).items():
                add(k, v)
        waits = []
        wd = self.waited[eng]
        for k, v in need.items():
            if wd.get(k, 0) >= v:
                continue
            wd[k] = v
            waits.append((k, v))
        return waits

    def _commit(self, tok, reads, writes):
        k, v = tok
        for r in reads:
            d = self.readers.setdefault(r, {})
            if d.get(k, 0) < v:
                d[k] = v
        for w in writes:
            self.last_w[w] = tok
            self.readers[w] = {}

    def op(self, eng, fn, reads=(), writes=()):
        waits = self._deps(eng, reads, writes)
        self.cnt[eng] += 1
        tok = (eng, self.cnt[eng])
        self.ops[eng].append((waits, fn, True))
        self._commit(tok, reads, writes)

    def dma(self, q, out, in_, reads=(), writes=(), key=None, **kw):
        pairs = out if isinstance(out, list) else [(out, in_)]
        waits = self._deps(q, reads, writes)
        dk = "d:" + key
        self.dcnt[dk] = self.dcnt.get(dk, 0) + 16 * len(pairs)
        tok = (dk, self.dcnt[dk])

        def fn(e, pairs=pairs, dk=dk, kw=kw):
            for o, i in pairs:
                e.dma_start(out=o, in_=i, **kw).then_inc(self.sems[dk], 16)

        self.ops[q].append((waits, fn, False))
        self._commit(tok, reads, writes)

    def fence(self):
        allw = [(e, self.cnt[e]) for e in self.ENGS if self.cnt[e] > 0] + list(self.dcnt.items())
        for eng in self.ENGS:
            waits = []
            wd = self.waited[eng]
            for k, v in allw:
                if k == eng:
                    continue
                if wd.get(k, 0) >= v:
                    continue
                wd[k] = v
                waits.append((k, v))
            if waits:
                self.ops[eng].append((waits, None, False))
        self.last_w = {}
        self.readers = {}

    def finish(self):
        nc = self.nc
        waits = [(k, v) for k, v in self.dcnt.items()] + [(e, self.cnt[e]) for e in self.ENGS if self.cnt[e] > 0 and e != "sp"]
        self.ops["sp"].append((waits, None, False))
        for e in self.ENGS:
            self.sems[e] = self.es.enter_context(nc.semaphore("sem_" + e))
        for i, dk in enumerate(self.dcnt):
            self.sems[dk] = self.es.enter_context(nc.semaphore("semd_%d" % i))

        def emit(name):
            def f(e):
                for waits, fn, inc in self.ops[name]:
                    for k, v in waits:
                        e.wait_ge(self.sems[k], v)
                    if fn is None:
                        continue
                    ins = fn(e)
                    if inc:
                        ins.then_inc(self.sems[name], 1)
            return f

        with nc.Block() as block:
            block.tensor(emit("pe"))
            block.scalar(emit("act"))
            block.vector(emit("dve"))
            block.gpsimd(emit("pool"))
            block.sync(emit("sp"))
        self.es.close()


S = 8192
D = 2048
NIN = 10240
EPS = 1e-6
NV = 176
V_CW, V_CB, V_BR, V_BI, V_LAM = 0, 64, 80, 112, 144


class Arena:
    def __init__(self, s, nbytes):
        self.t = s.sbuf("arena", [128, nbytes // 2], BF16)
        self.cap = nbytes // 2
        self.off = 0

    def mark(self):
        return self.off

    def reset(self, m):
        self.off = m

    def alloc(self, shape, dt):
        n = 1
        for d in shape:
            n *= d
        nel = n * 2 if dt == F32 else n
        nel = (nel + 15) // 16 * 16
        assert self.off + nel <= self.cap, ("arena overflow", self.off, nel, self.cap)
        ap = self.t[:, self.off:self.off + nel]
        self.off += nel
        if dt == F32:
            ap = ap.bitcast(F32)
        ap = ap[:, 0:n]
        if len(shape) == 2:
            ap = ap.rearrange("p (a b) -> p a b", a=shape[0])
        elif len(shape) == 3:
            ap = ap.rearrange("p (a b c) -> p a b c", a=shape[0], b=shape[1])
        elif len(shape) == 4:
            ap = ap.rearrange("p (a b c d) -> p a b c d", a=shape[0], b=shape[1], c=shape[2])
        return ap


def build_program(debug=False, phases=(0, 1, 2, 3, 4, 5), p3="AdaBC", p3g=8):
    nc = bass.Bass("TRN2", target_bir_lowering=False)
    ikind = "ExternalInput"
    skind = "ExternalOutput" if debug else "Internal"

    def din(name, shape, dt=F32):
        return nc.dram_tensor(name, shape, dt, kind=ikind).ap()

    def dscr(name, shape, dt=BF16):
        return nc.dram_tensor(name, shape, dt, kind=skind).ap()

    x = din("x", [S, D])
    p = din("p", [S, 256])
    w_in = din("w_in", [D, NIN])
    w_a = din("w_a", [D, D])
    w_b = din("w_b", [1024, D])
    w_out = din("w_out", [D, D])
    w_gate = din("w_gate", [D, D])
    w_ple = din("w_ple", [256, D])
    w_rg = din("w_rg", [4096, 128])
    w_ig = din("w_ig", [4096, 128])
    vecs = din("vecs", [128, NV])
    gpre_bc = din("gpre_bc", [128, D])
    gpost_bc = din("gpost_bc", [128, D])
    gple_bc = din("gple_bc", [128, D])
    ident = din("ident", [128, 128], BF16)
    f64t = din("f64t", [64, 128], BF16)
    cosG = din("cosG", [128, S], BF16)
    sinG = din("sinG", [128, S], BF16)
    nsinG = din("nsinG", [128, S], BF16)
    ccs = din("ccs", [128, 256], BF16)
    y = nc.dram_tensor("y", [S, D], F32, kind="ExternalOutput").ap()

    win_bf = dscr("win_bf", [D, NIN])
    wa_bf = dscr("wa_bf", [D, D])
    wb_bf = dscr("wb_bf", [1024, D])
    wout_bf = dscr("wout_bf", [D, D])
    wgate_bf = dscr("wgate_bf", [D, D])
    wple_bf = dscr("wple_bf", [256, D])
    wrg_bf = dscr("wrg_bf", [4096, 128])
    wig_bf = dscr("wig_bf", [4096, 128])
    XA_s = dscr("XA_s", [D, S])
    ZA_s = dscr("ZA_s", [D, S])
    XB_s = dscr("XB_s", [8, S, 128])
    ZB_s = dscr("ZB_s", [1024, S])
    GA_s = dscr("GA_s", [D, S])
    GB_s = dscr("GB_s", [D, S])
    YA_s = dscr("YA_s", [D, S])
    YB_s = dscr("YB_s", [1024, S])
    YA_t = YA_s.rearrange("a b -> (a b)").rearrange("(tt p kc t) -> tt p kc t", p=128, kc=16, t=512)
    YB_t = YB_s.rearrange("a b -> (a b)").rearrange("(tt p kc t) -> tt p kc t", p=128, kc=8, t=512)

    s = Sched(nc)
    A = Arena(s, 204 * 1024)
    PS = [s.psum("psb%d" % i, [128, 512], F32) for i in range(8)]

    def ps(i):
        return PS[i][:]

    def psbf(i):
        return PS[i][:].bitcast(BF16)

    vec_t = A.alloc([NV], F32)
    idt = A.alloc([128], BF16)
    hbr = A.alloc([32], F32)
    hbi = A.alloc([32], F32)
    sc4 = A.alloc([32], F32)
    mhalf = A.alloc([4], F32)
    s.dma("sp", vec_t, vecs, writes=["vec"], key="c0")
    s.dma("sp", idt, ident, writes=["idt"], key="c1")
    s.op("dve", lambda e: e.memset(mhalf, -0.5), writes=["mhalf"])
    s.op("dve", lambda e: e.tensor_scalar(out=hbr, in0=vec_t[:, V_BR:V_BR + 32], scalar1=0.5, scalar2=None, op0=ALU.mult),
         reads=["vec"], writes=["hbr"])
    s.op("dve", lambda e: e.tensor_scalar(out=hbi, in0=vec_t[:, V_BI:V_BI + 32], scalar1=0.5, scalar2=None, op0=ALU.mult),
         reads=["vec"], writes=["hbi"])
    s.op("act", lambda e: e.activation(out=sc4, in_=vec_t[:, V_LAM:V_LAM + 32], func=AF.Exp, scale=-1.0), reads=["vec"], writes=["sc4"])
    s.op("act", lambda e: e.activation(out=sc4, in_=sc4, func=AF.Ln, bias=1.0), reads=["sc4"], writes=["sc4"])
    s.op("dve", lambda e: e.tensor_scalar(out=sc4, in0=sc4, scalar1=-4.0, scalar2=None, op0=ALU.mult), reads=["sc4"], writes=["sc4"])
    base_mark = A.mark()

    def tiled(t, nkb):
        return t.rearrange("a b -> (a b)").rearrange("(g kb kp k8 n) -> g kb kp k8 n", kb=nkb, kp=128, k8=8, n=512)

    win_t = win_bf.rearrange("a b -> (a b)").rearrange("(g kp kc n) -> g kp kc n", kp=128, kc=16, n=512)
    wa_t, wb_t, wo_t, wg_t = tiled(wa_bf, 2), tiled(wb_bf, 1), tiled(wout_bf, 2), tiled(wgate_bf, 2)

    def ph0():
        def cast(dst, src, rows, cols):
            cw = min(cols, 2048)
            nseg = cols // cw
            rstep = 128 if nseg > 1 else 512
            pairs = []
            for r0 in range(0, rows, rstep):
                r1 = min(rows, r0 + rstep)
                if nseg > 1:
                    o = dst[r0:r1, :].rearrange("r (g c) -> r g c", g=nseg)
                    i = src[r0:r1, :].rearrange("r (g c) -> r g c", g=nseg)
                else:
                    o = dst[r0:r1, :]
                    i = src[r0:r1, :]
                pairs.append((o, i))
            s.dma("pool", pairs, None, writes=["W"], key="cast")

        sv_in = w_in.rearrange("(kc kp) (g n) -> g kp kc n", kp=128, n=512)
        s.dma("pool", [(win_t[g], sv_in[g]) for g in range(20)], None, writes=["W"], key="cast")
        def cast_tiled(dst_t, src, nkb):
            sv = src.rearrange("(kb k8 kp) (g n) -> g kb kp k8 n", k8=8, kp=128, n=512)
            pairs = [(dst_t[g, kb], sv[g, kb]) for g in range(4) for kb in range(nkb)]
            s.dma("pool", pairs, None, writes=["W"], key="cast")

        cast_tiled(wa_t, w_a, 2)
        cast_tiled(wb_t, w_b, 1)
        cast_tiled(wo_t, w_out, 2)
        cast_tiled(wg_t, w_gate, 2)
        cast(wple_bf, w_ple, 256, D)
        cast(wrg_bf, w_rg, 4096, 128)
        cast(wig_bf, w_ig, 4096, 128)
        s.fence()

    def ph1():
        A.reset(base_mark)
        gpre_t = A.alloc([D], F32)
        xts = [A.alloc([D], F32) for _ in range(3)]
        hbs = [A.alloc([D], BF16) for _ in range(2)]
        junk = A.alloc([D], BF16)
        TT1 = 1024
        NSUB = TT1 // 128
        hTs = [A.alloc([16, TT1], BF16) for _ in range(2)]
        wts = [A.alloc([16, 512], BF16) for _ in range(3)]
        stg = [A.alloc([4, TT1], BF16) for _ in range(4)]
        sst = [A.alloc([NSUB], F32) for _ in range(2)]
        tmt = [A.alloc([NSUB], F32) for _ in range(2)]
        rst = [A.alloc([NSUB], F32) for _ in range(2)]
        s.dma("sp", gpre_t, gpre_bc, writes=["gpre"], key="c2")
        win_v = win_bf.rearrange("(kc kp) n -> kp kc n", kp=128)
        cnt = {"x": 0, "w": 0, "stg": 0, "mm": 0, "tr": 0}
        order = [(g, "xa") for g in range(0, 4)] + [(8, "xb"), (9, "xb")] + [(g, "za") for g in range(4, 8)] + \
                [(10, "zb"), (11, "zb")] + [(g, "ga") for g in range(12, 16)] + [(g, "gb") for g in range(16, 20)]
        dst_fm = {"xa": (XA_s, 0), "za": (ZA_s, 4), "zb": (ZB_s, 10), "ga": (GA_s, 12), "gb": (GB_s, 16)}
        funcs = {"xa": None, "xb": None, "za": AF.Silu, "zb": AF.Silu, "ga": AF.Sigmoid, "gb": AF.Sigmoid}

        def prep(tt):
            t0 = tt * TT1
            hT = hTs[tt % 2]
            ss, tm, rs = sst[tt % 2], tmt[tt % 2], rst[tt % 2]
            for j in range(NSUB):
                xi = cnt["x"] % 3
                hi = cnt["x"] % 2
                cnt["x"] += 1
                xt, hb = xts[xi], hbs[hi]
                s.dma("sp", xt, x[t0 + j * 128:t0 + (j + 1) * 128, :], writes=[("xt", xi)], key="x%d" % xi)
                s.op("act", lambda e, xt=xt, ss=ss, j=j: e.activation(out=junk, in_=xt, func=AF.Square, accum_out=ss[:, j:j + 1]),
                     reads=[("xt", xi)], writes=["junk", ("ss", tt % 2, j)])
                s.op("dve", lambda e, ss=ss, tm=tm, j=j: e.tensor_scalar(out=tm[:, j:j + 1], in0=ss[:, j:j + 1], scalar1=1.0 / D, scalar2=EPS,
                                                                     op0=ALU.mult, op1=ALU.add),
                     reads=[("ss", tt % 2, j)], writes=[("tm", tt % 2, j)])
                s.op("pool", lambda e, rs=rs, tm=tm, j=j: e.tensor_tensor(out=rs[:, j:j + 1], in0=tm[:, j:j + 1], in1=mhalf[:, 0:1], op=ALU.pow),
                     reads=[("tm", tt % 2, j), "mhalf"], writes=[("rs", tt % 2, j)])
                s.op("dve", lambda e, xt=xt, hb=hb, rs=rs, j=j: e.scalar_tensor_tensor(out=hb, in0=xt, scalar=rs[:, j:j + 1], in1=gpre_t,
                                                                                    op0=ALU.mult, op1=ALU.mult),
                     reads=[("xt", xi), ("rs", tt % 2, j), "gpre"], writes=[("hb", hi)])
                for half in range(2):
                    bank = 6 + cnt["tr"] % 2
                    cnt["tr"] += 1

                    def trf(e, hb=hb, half=half, bank=bank):
                        ins = None
                        pb = psbf(bank)
                        for k in range(8):
                            kc = half * 8 + k
                            ins = e.transpose(pb[:, k * 128:(k + 1) * 128], hb[:, kc * 128:(kc + 1) * 128], idt)
                        return ins

                    s.op("pe", trf, reads=[("hb", hi), "idt"], writes=[("ps", bank)])
                    s.op("dve", lambda e, hT=hT, half=half, bank=bank, j=j: e.tensor_copy(
                        out=hT[:, half * 8:(half + 1) * 8, j * 128:(j + 1) * 128],
                        in_=psbf(bank).rearrange("p (k t) -> p k t", k=8)),
                        reads=[("ps", bank)], writes=[("hT", tt % 2)])

        def groups(tt, lo, hi_):
            t0 = tt * TT1
            hT = hTs[tt % 2]
            for (g, kind) in order[lo:hi_]:
                wi = cnt["w"] % 3
                cnt["w"] += 1
                wt = wts[wi]
                s.dma("sp", wt, win_t[g], reads=["W"], writes=[("wt", wi)], key="w%d" % wi)
                si = cnt["stg"] % 4
                if kind != "xb":
                    cnt["stg"] += 1
                    st = stg[si]
                    for c4 in range(4):
                        pr = cnt["mm"] % 3
                        cnt["mm"] += 1
                        bks = (2 * pr, 2 * pr + 1)

                        def mm(e, wt=wt, hT=hT, c4=c4, bks=bks):
                            ins = None
                            for kc in range(16):
                                for th in range(2):
                                    ins = e.matmul(ps(bks[th]), lhsT=wt[:, kc, c4 * 128:(c4 + 1) * 128], rhs=hT[:, kc, th * 512:(th + 1) * 512],
                                                   start=(kc == 0), stop=(kc == 15))
                            return ins

                        s.op("pe", mm, reads=[("wt", wi), ("hT", tt % 2)], writes=[("ps", bks[0]), ("ps", bks[1])])
                        fn = funcs[kind]
                        for th in range(2):
                            bank = bks[th]
                            if fn is None:
                                s.op("dve", lambda e, st=st, c4=c4, bank=bank, th=th: e.tensor_copy(out=st[:, c4, th * 512:(th + 1) * 512], in_=ps(bank)),
                                     reads=[("ps", bank)], writes=[("stg", si)])
                            else:
                                s.op("act", lambda e, st=st, c4=c4, bank=bank, fn=fn, th=th: e.activation(out=st[:, c4, th * 512:(th + 1) * 512], in_=ps(bank), func=fn),
                                     reads=[("ps", bank)], writes=[("stg", si)])
                    dst, g0 = dst_fm[kind]
                    c0 = (g - g0) * 512
                    s.dma("pool", dst[c0:c0 + 512, t0:t0 + TT1].rearrange("(j p) t -> p j t", p=128), st,
                          reads=[("stg", si)], writes=[], key="st%d" % si)
                else:
                    cb = g - 8
                    for j in range(NSUB):
                        si = cnt["stg"] % 4
                        cnt["stg"] += 1
                        st = stg[si]
                        bank = 2 * (cnt["mm"] % 3) + (j % 2)
                        if j % 2 == 1:
                            cnt["mm"] += 1

                        def mm(e, wt=wt, hT=hT, j=j, bank=bank):
                            ins = None
                            for kc in range(16):
                                ins = e.matmul(ps(bank), lhsT=hT[:, kc, j * 128:(j + 1) * 128], rhs=wt[:, kc, :],
                                               start=(kc == 0), stop=(kc == 15))
                            return ins

                        s.op("pe", mm, reads=[("wt", wi), ("hT", tt % 2)], writes=[("ps", bank)])
                        s.op("dve", lambda e, st=st, bank=bank: e.tensor_copy(out=st[:, 0, 0:512], in_=ps(bank)),
                             reads=[("ps", bank)], writes=[("stg", si)])
                        s.dma("pool", XB_s[cb * 4:(cb + 1) * 4, t0 + j * 128:t0 + (j + 1) * 128, :].rearrange("g p c -> p g c"),
                              st[:, 0, 0:512].rearrange("p (b c) -> p b c", c=128), reads=[("stg", si)], writes=[], key="st%d" % si)

        NT = S // TT1
        prep(0)
        for tt in range(NT):
            groups(tt, 0, 10)
            if tt + 1 < NT:
                prep(tt + 1)
            groups(tt, 10, 20)
        s.fence()

    def ph2():
        A.reset(base_mark)
        xa_pad = A.alloc([S + 4], BF16)
        zsq = [A.alloc([2048], BF16) for _ in range(2)]
        c2 = [A.alloc([S], BF16) for _ in range(2)]
        hf = A.alloc([S], F32)
        NWS = 3
        wsA = [A.alloc([2048], F32) for _ in range(NWS)]
        wsS = [A.alloc([2048], F32) for _ in range(NWS)]
        wsU = [A.alloc([2048], F32) for _ in range(NWS)]
        hbq = [A.alloc([2048], F32) for _ in range(2)]
        yst = [A.alloc([2048], BF16) for _ in range(2)]
        gw = [A.alloc([4, 128], BF16) for _ in range(2)]
        dg = [A.alloc([4, 128], BF16) for _ in range(2)]
        s.op("dve", lambda e: e.memset(xa_pad[:, 0:2], 0.0), writes=["xa"])
        s.op("dve", lambda e: e.memset(xa_pad[:, S + 2:S + 4], 0.0), writes=["xa"])
        cnt = {"u": 0, "gs": 0, "cv": 0, "ev": 0}
        units = []

        def pro_load(h):
            hb2 = h % 2
            s.dma("sp", xa_pad[:, 2:S + 2], XA_s[h * 128:(h + 1) * 128, :], writes=["xa"], key="xa")
            pairs = []
            for d in range(2):
                pairs.append((gw[hb2][:, d * 2 + 0, :], wrg_bf[(d * 16 + h) * 128:(d * 16 + h + 1) * 128, :]))
                pairs.append((gw[hb2][:, d * 2 + 1, :], wig_bf[(d * 16 + h) * 128:(d * 16 + h + 1) * 128, :]))
            s.dma("sp", pairs, None, writes=[("gw", hb2)], key="gw%d" % hb2)
            for k in range(4):
                s.op("dve", lambda e, k=k: e.tensor_scalar(out=dg[hb2][:, k, :], in0=idt, scalar1=vec_t[:, V_CW + k * 16 + h:V_CW + k * 16 + h + 1],
                                                      scalar2=None, op0=ALU.mult),
                     reads=["idt", "vec"], writes=[("dg", hb2)])

        def pro_conv(h, chs=range(16)):
            hb2 = h % 2
            cbuf = c2[hb2]
            for ch in chs:
                bank = cnt["cv"] % 2
                cnt["cv"] += 1

                def cv(e, ch=ch, bank=bank):
                    ins = None
                    for k in range(4):
                        ins = e.matmul(ps(bank), lhsT=dg[hb2][:, k, :], rhs=xa_pad[:, ch * 512 + k:ch * 512 + k + 512], start=(k == 0), stop=(k == 3))
                    return ins

                s.op("pe", cv, reads=[("dg", hb2), "xa"], writes=[("ps", bank)])
                if True:
                    s.op("dve", lambda e, ch=ch, bank=bank: e.tensor_scalar(out=cbuf[:, ch * 512:(ch + 1) * 512], in0=ps(bank),
                                                                     scalar1=vec_t[:, V_CB + h:V_CB + h + 1], scalar2=None, op0=ALU.add),
                         reads=[("ps", bank), "vec"], writes=[("c2", hb2, ch // 4)])
                else:
                    s.op("act", lambda e, ch=ch, bank=bank: e.activation(out=cbuf[:, ch * 512:(ch + 1) * 512], in_=ps(bank), func=AF.Identity,
                                                                  bias=vec_t[:, V_CB + h:V_CB + h + 1]),
                         reads=[("ps", bank), "vec"], writes=[("c2", hb2, ch // 4)])

        def make_unit(h, d, q, u):
            hb2 = h % 2
            cbuf = c2[hb2]
            vi = d * 16 + h
            ws = u % NWS
            Aw, Sw, Uw = wsA[ws], wsS[ws], wsU[ws]

            def s1():
                for cc in range(4):
                    slot = cnt["gs"] % 3
                    cnt["gs"] += 1
                    b0, b1 = 2 + slot * 2, 3 + slot * 2
                    tok0 = q * 2048 + cc * 512

                    def gm(e, b0=b0, b1=b1, tok0=tok0):
                        e.matmul(ps(b0), lhsT=gw[hb2][:, d * 2 + 0, :], rhs=cbuf[:, tok0:tok0 + 512], start=True, stop=True)
                        return e.matmul(ps(b1), lhsT=gw[hb2][:, d * 2 + 1, :], rhs=cbuf[:, tok0:tok0 + 512], start=True, stop=True)

                    s.op("pe", gm, reads=[("gw", hb2), ("c2", hb2, q)], writes=[("ps", b0), ("ps", b1)])
                    s.op("act", lambda e, cc=cc, b0=b0: e.activation(out=Aw[:, cc * 512:(cc + 1) * 512], in_=ps(b0), func=AF.Tanh,
                                                               scale=0.5, bias=hbr[:, vi:vi + 1]),
                         reads=[("ps", b0), "hbr"], writes=[("A", ws)])
                    s.op("act", lambda e, cc=cc, b1=b1: e.activation(out=Uw[:, cc * 512:(cc + 1) * 512], in_=ps(b1), func=AF.Tanh,
                                                               scale=0.5, bias=hbi[:, vi:vi + 1]),
                         reads=[("ps", b1), "hbi"], writes=[("U", ws)])
                s.op("act", lambda e: e.activation(out=Aw, in_=Aw, func=AF.Exp, scale=sc4[:, vi:vi + 1], bias=sc4[:, vi:vi + 1]),
                     reads=[("A", ws), "sc4"], writes=[("A", ws)])
                s.op("act", lambda e: e.activation(out=Sw, in_=Aw, func=AF.Square), reads=[("A", ws)], writes=[("S", ws)])
                s.op("dve", lambda e: e.scalar_tensor_tensor(out=Uw, in0=Uw, scalar=1.0, in1=cbuf[:, q * 2048:(q + 1) * 2048],
                                                             op0=ALU.add, op1=ALU.mult),
                     reads=[("U", ws), ("c2", hb2, q)], writes=[("U", ws)])

            def s2():
                s.op("act", lambda e: e.activation(out=Sw, in_=Sw, func=AF.Sqrt, scale=-1.0, bias=1.0), reads=[("S", ws)], writes=[("S", ws)])
                s.op("dve", lambda e: e.scalar_tensor_tensor(out=Uw, in0=Uw, scalar=0.5, in1=Sw, op0=ALU.mult, op1=ALU.mult),
                     reads=[("U", ws), ("S", ws)], writes=[("U", ws)])
                if d == 0:
                    init = 0.0 if q == 0 else hf[:, q * 2048 - 1:q * 2048]
                    rd = [("A", ws), ("U", ws)] + ([] if q == 0 else [("hf", q - 1)])
                    s.op("dve", lambda e: e.tensor_tensor_scan(out=hf[:, q * 2048:(q + 1) * 2048], data0=Aw, data1=Uw,
                                                               initial=init, op0=ALU.mult, op1=ALU.add),
                         reads=rd, writes=[("hf", q)])
                else:
                    hi = u % 2
                    hq = hbq[hi]
                    init = 0.0 if q == 3 else hbq[1 - hi][:, 0:1]
                    rd = [("A", ws), ("U", ws)] + ([] if q == 3 else [("hbq", 1 - hi)])
                    s.op("dve", lambda e: e.tensor_tensor_scan(out=hq[:, ::-1], data0=Aw[:, ::-1], data1=Uw[:, ::-1],
                                                               initial=init, op0=ALU.mult, op1=ALU.add),
                         reads=rd, writes=[("hbq", hi)])
                    zi_ = u % 2
                    s.dma("sp", zsq[zi_], ZA_s[h * 128:(h + 1) * 128, q * 2048:(q + 1) * 2048], writes=[("zsq", zi_)], key="zs%d" % zi_)
                    s.op("pool", lambda e: e.tensor_tensor(out=Uw, in0=hq, in1=hf[:, q * 2048:(q + 1) * 2048], op=ALU.add),
                         reads=[("hbq", hi), ("hf", q)], writes=[("U", ws)])
                    s.op("pool", lambda e: e.tensor_tensor(out=yst[zi_], in0=Uw, in1=zsq[zi_], op=ALU.mult),
                         reads=[("U", ws), ("zsq", zi_)], writes=[("yst", zi_)])
                    s.dma("pool", YA_t[q * 4:(q + 1) * 4, :, h, :].rearrange("tt p t -> p tt t"), yst[zi_].rearrange("p (tt t) -> p tt t", tt=4),
                          reads=[("yst", zi_)], writes=[], key="ys%d" % zi_)

            return s1, s2

        for h in range(16):
            first = True
            for d in range(2):
                qs = [0, 1, 2, 3] if d == 0 else [3, 2, 1, 0]
                for q in qs:
                    u = cnt["u"]
                    cnt["u"] += 1
                    s1, s2 = make_unit(h, d, q, u)
                    units.append((h if first else None, s1, s2))
                    first = False
        CONV_SPLIT = {1: [0, 1, 2], 2: [3, 4, 5], 3: [6, 7], 4: [8, 9], 5: [10, 11], 6: [12, 13], 7: [14, 15]}
        pro_load(0)
        pro_conv(0)
        pro_load(1)
        for i, (hp, s1, s2) in enumerate(units):
            hh, idx = i // 8, i % 8
            s1()
            if idx >= 1 and hh + 1 < 16:
                pro_conv(hh + 1, CONV_SPLIT[idx])
            if idx == 7 and hh + 2 < 16:
                pro_load(hh + 2)
            if i >= 1:
                units[i - 1][2]()
        units[-1][2]()
        s.fence()

    def ph3():
        A.reset(base_mark)
        Xg = A.alloc([128, 128], BF16)
        Zre = A.alloc([64, 128], BF16)
        Zim = A.alloc([64, 128], BF16)
        cG = A.alloc([64, 128], BF16)
        sG = A.alloc([64, 128], BF16)
        nG = A.alloc([64, 128], BF16)
        ccs_t = A.alloc([256], BF16)
        f64_t = A.alloc([128], BF16)
        Yt = [A.alloc([4, 2, 128], BF16) for _ in range(2)]
        szb = A.alloc([S], BF16)
        ybt = A.alloc([S], BF16)
        s.dma("sp", cG.rearrange("p a b -> p (a b)"), cosG, writes=["cG"], key="t0")
        s.dma("sp", sG.rearrange("p a b -> p (a b)"), sinG, writes=["sG"], key="t1")
        s.dma("sp", nG.rearrange("p a b -> p (a b)"), nsinG, writes=["nG"], key="t2")
        s.dma("sp", ccs_t, ccs, writes=["ccs"], key="t3")
        s.dma("sp", f64_t[0:64, :], f64t, writes=["f64"], key="t4")
        cnt = {"a": 0, "b": 0, "c": 0, "ev": 0}
        for g in range(p3g):
            s.dma("sp", Xg[0:64].rearrange("p b c -> p (b c)"), XB_s[g].rearrange("(a b) c -> a (b c)", a=64), writes=["Xg"], key="xg")
            s.dma("sp", szb, ZB_s[g * 128:(g + 1) * 128, :], writes=["szb"], key="szb")
            for c4 in range(32 if "A" in p3 else 0):
                bank = cnt["a"] % 2
                cnt["a"] += 1

                def sa(e, c4=c4, bank=bank):
                    ins = None
                    for cc in range(4):
                        c = c4 * 4 + cc
                        ins = e.matmul(ps(bank)[:, cc * 128:(cc + 1) * 128], lhsT=Xg[0:64, :, c], rhs=f64_t[0:64, :], start=True, stop=True)
                    return ins

                s.op("pe", sa, reads=["Xg", "f64"], writes=[("ps", bank)])
                pv = ps(bank).rearrange("p (cc ri a) -> p ri a cc", cc=4, ri=2, a=64)
                if "d" in p3:
                    s.op("dve", lambda e, c4=c4, pv=pv: e.tensor_copy(out=Zre[:, :, c4 * 4:(c4 + 1) * 4], in_=pv[:, 0]),
                         reads=[("ps", bank)], writes=["Zre"])
                if "a" in p3:
                    s.op("dve", lambda e, c4=c4, pv=pv: e.tensor_copy(out=Zim[:, :, c4 * 4:(c4 + 1) * 4], in_=pv[:, 1]),
                         reads=[("ps", bank)], writes=["Zim"])
            for a2 in range(32 if "B" in p3 else 0):
                bank = 2 + cnt["b"] % 4
                cnt["b"] += 1

                def sb(e, a2=a2, bank=bank):
                    ins = None
                    for ai in range(2):
                        ap_ = 2 * a2 + ai
                        o = ai * 256
                        e.matmul(ps(bank)[:, o:o + 128], lhsT=Zre[:, ap_, :], rhs=cG[:, ap_, :], start=True, stop=False)
                        e.matmul(ps(bank)[:, o:o + 128], lhsT=Zim[:, ap_, :], rhs=sG[:, ap_, :], start=False, stop=True)
                        e.matmul(ps(bank)[:, o + 128:o + 256], lhsT=Zre[:, ap_, :], rhs=nG[:, ap_, :], start=True, stop=False)
                        ins = e.matmul(ps(bank)[:, o + 128:o + 256], lhsT=Zim[:, ap_, :], rhs=cG[:, ap_, :], start=False, stop=True)
                    return ins

                s.op("pe", sb, reads=["Zre", "Zim", "cG", "sG", "nG"], writes=[("ps", bank)])
                a4 = a2 // 2
                yi = a4 % 2
                Y = Yt[yi]
                half = a2 % 2
                if a2 % 2 == 0:
                    s.op("dve", lambda e, Y=Y, bank=bank, half=half: e.tensor_copy(out=Y[:, half * 2:half * 2 + 2].rearrange("p a r b -> p (a r b)"), in_=ps(bank)),
                         reads=[("ps", bank)], writes=[("Y", yi)])
                else:
                    s.op("act", lambda e, Y=Y, bank=bank, half=half: e.activation(out=Y[:, half * 2:half * 2 + 2].rearrange("p a r b -> p (a r b)"), in_=ps(bank), func=AF.Copy),
                         reads=[("ps", bank)], writes=[("Y", yi)])
                if a2 % 2 == 1 and "C" in p3:
                    cbank = 6 + cnt["c"] % 2
                    cnt["c"] += 1

                    def sc(e, Y=Y, cbank=cbank):
                        ins = None
                        for al in range(4):
                            e.matmul(ps(cbank)[:, al * 128:(al + 1) * 128], lhsT=ccs_t[:, 0:128], rhs=Y[:, al, 0, :], start=True, stop=False)
                            ins = e.matmul(ps(cbank)[:, al * 128:(al + 1) * 128], lhsT=ccs_t[:, 128:256], rhs=Y[:, al, 1, :], start=False, stop=True)
                        return ins

                    s.op("pe", sc, reads=[("Y", yi), "ccs"], writes=[("ps", cbank)])
                    zv = szb.rearrange("p (b a) -> p a b", a=64)[:, a4 * 4:a4 * 4 + 4, :]
                    ov = ybt.rearrange("p (b a) -> p a b", a=64)[:, a4 * 4:a4 * 4 + 4, :]
                    s.op("dve", lambda e, zv=zv, ov=ov, cbank=cbank: e.tensor_tensor(out=ov, in0=ps(cbank).rearrange("p (a b) -> p a b", a=4), in1=zv, op=ALU.mult),
                         reads=[("ps", cbank), "szb"], writes=["ybt"])
            s.dma("pool", YB_t[:, :, g, :].rearrange("tt p t -> p tt t"), ybt.rearrange("p (tt t) -> p tt t", tt=16), reads=["ybt"], writes=[], key="yb")
        s.fence()

    X1_s = dscr("X1_s", [S, D], F32)

    class Ring:
        def __init__(self, nslots):
            self.slots = [A.alloc([8, 512], BF16) for _ in range(nslots)]
            self.n = nslots
            self.c = 0

        def load(self, view, nk, c0, tag):
            ids = []
            for b in range(nk // 8):
                i = self.c % self.n
                self.c += 1
                s.dma("sp", self.slots[i], view[c0 // 512, b], reads=["W"], writes=[("ws", i)], key="%s%d" % (tag, i))
                ids.append(i)
            return ids

        def w(self, ids, kc):
            return self.slots[ids[kc // 8]][:, kc % 8, :]

        def res(self, ids):
            return [("ws", i) for i in ids]

    EPS4 = EPS
    def ph4():
        A.reset(base_mark)
        ya_t = A.alloc([16, 512], BF16)
        yb_t = A.alloc([8, 512], BF16)
        gat = [A.alloc([4, 512], BF16) for _ in range(2)]
        gbt = [A.alloc([4, 512], BF16) for _ in range(2)]
        mxs = [A.alloc([16, 512], BF16)] * 2
        ring = Ring(6)
        t1 = [A.alloc([512], F32) for _ in range(2)]
        t2 = [A.alloc([512], F32) for _ in range(2)]
        xt = [A.alloc([D], F32) for _ in range(4)]
        oe = [A.alloc([D], F32) for _ in range(4)]
        junk = A.alloc([D], BF16)
        gpost_t = A.alloc([D], F32)
        ssO = [A.alloc([16], F32) for _ in range(2)]
        m1 = A.alloc([4], F32)
        tmo = A.alloc([4], F32)
        rso = A.alloc([4], F32)
        s.dma("sp", gpost_t, gpost_bc, writes=["gpost"], key="c3")
        wa_v = wa_t
        wb_v = wb_t
        wo_v = wo_t
        ya_v = YA_s.rearrange("(kc p) t -> p kc t", p=128)
        yb_v = YB_s.rearrange("(kc p) t -> p kc t", p=128)
        cnt = {"pair": 0, "mm": 0, "t": 0, "g": 0, "x": 0}

        def chain_a(tt, sso):
            for j in range(4):
                s.op("dve", lambda e, j=j: e.tensor_reduce(out=m1[:, j:j + 1], in_=sso[:, j * 4:(j + 1) * 4], axis=mybir.AxisListType.X, op=ALU.add),
                     reads=[("ssO", tt % 2, j, cb) for cb in range(4)], writes=[("m1", j)])
                s.op("dve", lambda e, j=j: e.tensor_scalar(out=tmo[:, j:j + 1], in0=m1[:, j:j + 1], scalar1=1.0 / D, scalar2=EPS4, op0=ALU.mult, op1=ALU.add),
                     reads=[("m1", j)], writes=[("tmo", j)])
                s.op("pool", lambda e, j=j: e.tensor_tensor(out=rso[:, j:j + 1], in0=tmo[:, j:j + 1], in1=mhalf[:, 0:1], op=ALU.pow),
                     reads=[("tmo", j), "mhalf"], writes=[("rso", j)])

        def chain_b(tt):
            t0 = tt * 512
            for j in range(4):
                s.op("dve", lambda e, j=j: e.scalar_tensor_tensor(out=oe[j], in0=oe[j], scalar=rso[:, j:j + 1], in1=gpost_t, op0=ALU.mult, op1=ALU.mult),
                     reads=[("oe", j), ("rso", j), "gpost"], writes=[("oe", j)])
                s.op("pool", lambda e, j=j: e.tensor_tensor(out=xt[j], in0=oe[j], in1=xt[j], op=ALU.add),
                     reads=[("oe", j), ("xt", j)], writes=[("xt", j)])
                s.dma("pool", X1_s[t0 + j * 128:t0 + (j + 1) * 128, :], xt[j], reads=[("xt", j)], writes=[], key="x1s%d" % j)

        for tt in range(S // 512):
            t0 = tt * 512
            mx = mxs[tt % 2]
            sso = ssO[tt % 2]
            s.dma("sp", ya_t, YA_t[tt], writes=["ya_t"], key="ya")
            s.dma("sp", yb_t, YB_t[tt], writes=["yb_t"], key="yb_t")
            for grp in range(4):
                ia = ring.load(wa_v, 16, grp * 512, "ra")
                ib = ring.load(wb_v, 8, grp * 512, "ra")
                gi = cnt["g"] % 2
                cnt["g"] += 1
                s.dma("sp", gat[gi], GA_s[grp * 512:(grp + 1) * 512, t0:t0 + 512].rearrange("(c p) t -> p c t", p=128), writes=[("gat", gi)], key="ga%d" % gi)
                s.dma("sp", gbt[gi], GB_s[grp * 512:(grp + 1) * 512, t0:t0 + 512].rearrange("(c p) t -> p c t", p=128), writes=[("gbt", gi)], key="gb%d" % gi)
                for c4 in range(4):
                    n = grp * 4 + c4
                    pr = cnt["pair"] % 2
                    cnt["pair"] += 1
                    bA, bB = 2 * pr, 2 * pr + 1

                    def ab(e, ia=ia, ib=ib, c4=c4, bA=bA, bB=bB):
                        ins = None
                        for kc in range(16):
                            e.matmul(ps(bA), lhsT=ring.w(ia, kc)[:, c4 * 128:(c4 + 1) * 128], rhs=ya_t[:, kc, :], start=(kc == 0), stop=(kc == 15))
                        for kc in range(8):
                            ins = e.matmul(ps(bB), lhsT=ring.w(ib, kc)[:, c4 * 128:(c4 + 1) * 128], rhs=yb_t[:, kc, :], start=(kc == 0), stop=(kc == 7))
                        return ins

                    s.op("pe", ab, reads=ring.res(ia) + ring.res(ib) + ["ya_t", "yb_t"], writes=[("ps", bA), ("ps", bB)])
                    ti = cnt["t"] % 2
                    cnt["t"] += 1
                    s.op("dve", lambda e, ti=ti, bA=bA, gi=gi, c4=c4: e.tensor_tensor(out=t1[ti], in0=ps(bA), in1=gat[gi][:, c4, :], op=ALU.mult),
                         reads=[("ps", bA), ("gat", gi)], writes=[("t1", ti)])
                    s.op("dve", lambda e, ti=ti, bB=bB, gi=gi, c4=c4: e.tensor_tensor(out=t2[ti], in0=ps(bB), in1=gbt[gi][:, c4, :], op=ALU.mult),
                         reads=[("ps", bB), ("gbt", gi)], writes=[("t2", ti)])
                    s.op("pool", lambda e, ti=ti, n=n, mx=mx: e.tensor_tensor(out=mx[:, n, :], in0=t1[ti], in1=t2[ti], op=ALU.add),
                         reads=[("t1", ti), ("t2", ti)], writes=[("mx", 0)])
                if grp == 0 and tt > 0:
                    chain_b(tt - 1)
            for j in range(4):
                s.dma("sp", xt[j], x[t0 + j * 128:t0 + (j + 1) * 128, :], writes=[("xt", j)], key="px%d" % j)
            for cb in range(4):
                io = ring.load(wo_v, 16, cb * 512, "ra")
                for j in range(4):
                    bank = 4 + cnt["mm"] % 4
                    cnt["mm"] += 1

                    def om(e, io=io, j=j, bank=bank, mx=mx):
                        ins = None
                        for kc in range(16):
                            ins = e.matmul(ps(bank), lhsT=mx[:, kc, j * 128:(j + 1) * 128], rhs=ring.w(io, kc), start=(kc == 0), stop=(kc == 15))
                        return ins

                    s.op("pe", om, reads=ring.res(io) + [("mx", 0)], writes=[("ps", bank)])
                    s.op("dve", lambda e, bank=bank, j=j, cb=cb: e.tensor_copy(out=oe[j][:, cb * 512:(cb + 1) * 512], in_=ps(bank)),
                         reads=[("ps", bank)], writes=[("oe", j)])
                    s.op("act", lambda e, j=j, cb=cb, sso=sso: e.activation(out=junk[:, 0:512], in_=oe[j][:, cb * 512:(cb + 1) * 512], func=AF.Square,
                                                                      accum_out=sso[:, j * 4 + cb:j * 4 + cb + 1]),
                         reads=[("oe", j)], writes=["junk", ("ssO", tt % 2, j, cb)])
            if debug and tt == 0:
                dbg_o = nc.dram_tensor("dbg_o", [128, D], F32, kind="ExternalOutput").ap()
                dbg_mx = nc.dram_tensor("dbg_mx", [128, 16 * 512], BF16, kind="ExternalOutput").ap()
                dbg_ss = nc.dram_tensor("dbg_ss", [128, 16], F32, kind="ExternalOutput").ap()
                s.dma("sp", dbg_o, oe[0], reads=[("oe", 0)], writes=[], key="dbg0")
                s.dma("sp", dbg_mx, mx.rearrange("p a b -> p (a b)"), reads=[("mx", 0)], writes=[], key="dbg1")
                s.dma("sp", dbg_ss, sso, reads=[("ssO", 0, jj, cc_) for jj in range(4) for cc_ in range(4)], writes=[], key="dbg2")
            chain_a(tt, sso)
        chain_b(S // 512 - 1)
        s.fence()

    def ph5():
        A.reset(base_mark)
        x1Ts = [A.alloc([16, 512], BF16) for _ in range(2)]
        pTs = [A.alloc([2, 512], BF16) for _ in range(2)]
        xp = [A.alloc([D], F32) for _ in range(2)]
        xq = [A.alloc([D], F32) for _ in range(4)]
        x1bs = [A.alloc([D], BF16) for _ in range(2)]
        oe = [A.alloc([D], F32) for _ in range(4)]
        ring = Ring(6)
        wP = A.alloc([2, D], BF16)
        sgt = [A.alloc([512], F32) for _ in range(2)]
        junk = A.alloc([D], BF16)
        gple_t = A.alloc([D], F32)
        ptf = [A.alloc([256], F32) for _ in range(2)]
        ptb = [A.alloc([256], BF16) for _ in range(2)]
        ss2 = A.alloc([4], F32)
        tm2 = A.alloc([4], F32)
        rs2 = A.alloc([4], F32)
        s.dma("sp", gple_t, gple_bc, writes=["gple"], key="c4")
        s.dma("sp", wP, wple_bf.rearrange("(kc kp) n -> kp kc n", kp=128), reads=["W"], writes=["wP"], key="c5")
        wg_v = wg_t
        cnt = {"mm": 0, "sg": 0, "x": 0, "q": 0, "tr": 0, "p": 0}

        def prep(tt):
            t0 = tt * 512
            x1T = x1Ts[tt % 2]
            pT = pTs[tt % 2]
            for j in range(4):
                xi = cnt["x"] % 2
                cnt["x"] += 1
                s.dma("sp", xp[xi], X1_s[t0 + j * 128:t0 + (j + 1) * 128, :], writes=[("xp", xi)], key="xp%d" % xi)
                s.op("act", lambda e, xi=xi: e.activation(out=x1bs[xi], in_=xp[xi], func=AF.Copy), reads=[("xp", xi)], writes=[("x1b", xi)])
                for half in range(2):
                    bank = 6 + cnt["tr"] % 2
                    cnt["tr"] += 1

                    def trf(e, half=half, xi=xi, bank=bank):
                        ins = None
                        pb = psbf(bank)
                        for k in range(8):
                            kc = half * 8 + k
                            ins = e.transpose(pb[:, k * 128:(k + 1) * 128], x1bs[xi][:, kc * 128:(kc + 1) * 128], idt)
                        return ins

                    s.op("pe", trf, reads=[("x1b", xi), "idt"], writes=[("ps", bank)])
                    s.op("dve", lambda e, half=half, j=j, bank=bank, x1T=x1T: e.tensor_copy(out=x1T[:, half * 8:(half + 1) * 8, j * 128:(j + 1) * 128],
                                                                                   in_=psbf(bank).rearrange("p (k t) -> p k t", k=8)),
                         reads=[("ps", bank)], writes=[("x1T", tt % 2)])
                pi = cnt["p"] % 2
                cnt["p"] += 1
                s.dma("sp", ptf[pi], p[t0 + j * 128:t0 + (j + 1) * 128, :], writes=[("ptf", pi)], key="pp%d" % pi)
                s.op("act", lambda e, pi=pi: e.activation(out=ptb[pi], in_=ptf[pi], func=AF.Copy), reads=[("ptf", pi)], writes=[("ptb", pi)])
                bank = 6 + cnt["tr"] % 2
                cnt["tr"] += 1

                def trp(e, pi=pi, bank=bank):
                    pb = psbf(bank)
                    e.transpose(pb[:, 0:128], ptb[pi][:, 0:128], idt)
                    return e.transpose(pb[:, 128:256], ptb[pi][:, 128:256], idt)

                s.op("pe", trp, reads=[("ptb", pi), "idt"], writes=[("ps", bank)])
                s.op("dve", lambda e, j=j, bank=bank, pT=pT: e.tensor_copy(out=pT[:, :, j * 128:(j + 1) * 128], in_=psbf(bank)[:, 0:256].rearrange("p (k t) -> p k t", k=2)),
                     reads=[("ps", bank)], writes=[("pT", tt % 2)])

        def gate(tt):
            t0 = tt * 512
            x1T = x1Ts[tt % 2]
            pT = pTs[tt % 2]
            for cb in range(4):
                ig = ring.load(wg_v, 16, cb * 512, "rb")
                if cb == 2:
                    for j in range(4):
                        s.dma("sp", xq[j], X1_s[t0 + j * 128:t0 + (j + 1) * 128, :], writes=[("xq", j)], key="xq%d" % j)
                for j in range(4):
                    bank = cnt["mm"] % 6
                    cnt["mm"] += 1
                    bank2 = cnt["mm"] % 6
                    cnt["mm"] += 1

                    def gm(e, ig=ig, j=j, bank=bank, x1T=x1T):
                        ins = None
                        for kc in range(16):
                            ins = e.matmul(ps(bank), lhsT=x1T[:, kc, j * 128:(j + 1) * 128], rhs=ring.w(ig, kc), start=(kc == 0), stop=(kc == 15))
                        return ins

                    def pm(e, j=j, cb=cb, bank2=bank2, pT=pT):
                        e.matmul(ps(bank2), lhsT=pT[:, 0, j * 128:(j + 1) * 128], rhs=wP[:, 0, cb * 512:(cb + 1) * 512], start=True, stop=False)
                        return e.matmul(ps(bank2), lhsT=pT[:, 1, j * 128:(j + 1) * 128], rhs=wP[:, 1, cb * 512:(cb + 1) * 512], start=False, stop=True)

                    s.op("pe", gm, reads=ring.res(ig) + [("x1T", tt % 2)], writes=[("ps", bank)])
                    s.op("pe", pm, reads=[("pT", tt % 2), "wP"], writes=[("ps", bank2)])
                    si = cnt["sg"] % 2
                    cnt["sg"] += 1
                    s.op("act", lambda e, si=si, bank=bank: e.activation(out=sgt[si], in_=ps(bank), func=AF.Sigmoid),
                         reads=[("ps", bank)], writes=[("sgt", si)])
                    s.op("dve", lambda e, si=si, bank2=bank2, j=j, cb=cb: e.tensor_tensor(out=oe[j][:, cb * 512:(cb + 1) * 512], in0=ps(bank2), in1=sgt[si], op=ALU.mult),
                         reads=[("ps", bank2), ("sgt", si)], writes=[("oe", j)])
        def gate_final(tt):
            t0 = tt * 512
            for j in range(4):
                s.op("act", lambda e, j=j: e.activation(out=junk, in_=oe[j], func=AF.Square, accum_out=ss2[:, j:j + 1]),
                     reads=[("oe", j)], writes=["junk", ("ss2", j)])
                s.op("dve", lambda e, j=j: e.tensor_scalar(out=tm2[:, j:j + 1], in0=ss2[:, j:j + 1], scalar1=1.0 / D, scalar2=EPS, op0=ALU.mult, op1=ALU.add),
                     reads=[("ss2", j)], writes=[("tm2", j)])
                s.op("pool", lambda e, j=j: e.tensor_tensor(out=rs2[:, j:j + 1], in0=tm2[:, j:j + 1], in1=mhalf[:, 0:1], op=ALU.pow),
                     reads=[("tm2", j), "mhalf"], writes=[("rs2", j)])
            for j in range(4):
                qi = j
                s.op("dve", lambda e, j=j: e.scalar_tensor_tensor(out=oe[j], in0=oe[j], scalar=rs2[:, j:j + 1], in1=gple_t, op0=ALU.mult, op1=ALU.mult),
                     reads=[("oe", j), ("rs2", j), "gple"], writes=[("oe", j)])
                s.op("pool", lambda e, j=j, qi=qi: e.tensor_tensor(out=xq[qi], in0=oe[j], in1=xq[qi], op=ALU.add),
                     reads=[("oe", j), ("xq", qi)], writes=[("xq", qi)])
                s.dma("pool", y[t0 + j * 128:t0 + (j + 1) * 128, :], xq[qi], reads=[("xq", qi)], writes=[], key="yo%d" % qi)

        NT = S // 512
        prep(0)
        prep(1)
        for tt in range(NT):
            gate(tt)
            if tt + 2 < NT:
                prep(tt + 2)
            gate_final(tt)
        s.fence()


    for _n, _f in enumerate((ph0, ph1, ph2, ph3, ph4, ph5)):
        if _n in phases:
            _f()
    s.finish()
    return nc


def _tables():
    bf = ml_dtypes.bfloat16
    ident = np.eye(128, dtype=np.float32).astype(bf)
    a = np.arange(64)
    ang = 2.0 * np.pi * np.outer(a, a) / 64.0
    f64t = np.concatenate([np.cos(ang), -np.sin(ang)], axis=1).astype(np.float32).astype(bf)
    b = np.arange(128, dtype=np.int64)[:, None, None]
    ap = np.arange(64, dtype=np.int64)[None, :, None]
    bp = np.arange(128, dtype=np.int64)[None, None, :]
    k = ap + 64 * bp
    ph = 2.0 * np.pi * ((b * k) % S).astype(np.float64) / S
    cosG = np.cos(ph).reshape(128, S).astype(np.float32).astype(bf)
    sinG = np.sin(ph).reshape(128, S).astype(np.float32).astype(bf)
    nsinG = (-np.sin(ph)).reshape(128, S).astype(np.float32).astype(bf)
    c = np.arange(128)
    angc = 2.0 * np.pi * np.outer(c, c) / 128.0
    scale = 1.0 / 1024.0
    ccs = np.concatenate([np.cos(angc) * scale, np.sin(angc) * scale], axis=1).astype(np.float32).astype(bf)
    return dict(ident=ident, f64t=f64t, cosG=cosG, sinG=sinG, nsinG=nsinG, ccs=ccs)


def _shared_inputs(inp):
    f = np.float32

    def fm(v):
        return np.ascontiguousarray(np.asarray(v, f).reshape(16, 128).T)

    cw = np.asarray(inp["conv_w"], f)[0]
    cols = [fm(cw[k]) for k in range(4)]
    cols.append(fm(np.asarray(inp["conv_b"], f)[0]))
    for name in ("b_rgate", "b_igate", "lam"):
        v = np.asarray(inp[name], f)[0]
        cols += [fm(v[0]), fm(v[1])]
    vecs = np.ascontiguousarray(np.concatenate(cols, axis=1))
    assert vecs.shape == (128, NV)

    def bc(v):
        return np.ascontiguousarray(np.broadcast_to(np.asarray(v, f).reshape(1, D), (128, D)))

    sh = dict(
        w_in=np.ascontiguousarray(np.asarray(inp["w_in"], f)[0]),
        w_a=np.ascontiguousarray(np.asarray(inp["w_a"], f)[0]),
        w_b=np.ascontiguousarray(np.asarray(inp["w_b"], f)[0]),
        w_out=np.ascontiguousarray(np.asarray(inp["w_out"], f)[0]),
        w_gate=np.ascontiguousarray(np.asarray(inp["w_ple_gate"], f)[0]),
        w_ple=np.ascontiguousarray(np.asarray(inp["w_ple"], f)[0]),
        w_rg=np.ascontiguousarray(np.asarray(inp["w_rgate"], f)[0].reshape(4096, 128)),
        w_ig=np.ascontiguousarray(np.asarray(inp["w_igate"], f)[0].reshape(4096, 128)),
        vecs=vecs,
        gpre_bc=bc(inp["g_pre"][0]),
        gpost_bc=bc(inp["g_post"][0]),
        gple_bc=bc(inp["g_ple"][0]),
    )
    sh.update(_tables())
    return sh


def kernel(**inp):
    f = np.float32
    xp = np.asarray(inp["x_prompt"], f)
    xs = np.asarray(inp["x_sample"], f)
    pp = np.asarray(inp["p_prompt"], f)[0]
    psm = np.asarray(inp["p_sample"], f)[0]
    seqs = [(xp[i], pp[i]) for i in range(4)] + [(xs[0], psm[0])]
    sh = _shared_inputs(inp)
    in_maps = []
    for c in range(8):
        xi, pi = seqs[c] if c < 5 else seqs[c - 5]
        m = dict(sh)
        m["x"] = np.ascontiguousarray(xi)
        m["p"] = np.ascontiguousarray(pi)
        in_maps.append(m)
    nc = build_program()
    res = run_bass_kernel_spmd(nc, in_maps, core_ids=list(range(8)))
    outs = [np.asarray(res.results[c]["y"], f).reshape(S, D) for c in range(5)]
    y_prompt = np.stack(outs[:4], axis=0)
    y_sample = outs[4][None]
    return (y_prompt, y_sample)
```
